# Optimizing a Trainium2 kernel written in Bass

```python
import jax, jax.numpy as jnp
from jax import lax
import numpy as np

D_MODEL = 2048
BATCH = 4
SEQ = 4096
DEPTH = 2

N_SELF = DEPTH // 2
N_CROSS = DEPTH - N_SELF
CONV_WIDTH = 3
HEAD_DIM = 64
N_HEADS = D_MODEL // HEAD_DIM
N_KV_HEADS = max(1, N_HEADS // 8)
GROUP = N_HEADS // N_KV_HEADS
WINDOW = 128
BLOCK = 128
ROT_DIM = HEAD_DIM // 4
ROPE_THETA = 500000.0
D_FF = 4 * D_MODEL
EPS = 1e-6

kernel_name = "yoco_shortconv_swa_sink_hybrid"


def rms_norm(x, g):
    xf = x.astype(jnp.float32)
    y = xf * lax.rsqrt(jnp.mean(xf * xf, axis=-1, keepdims=True) + EPS)
    return (y * g.astype(jnp.float32)).astype(x.dtype)


def ada_mod(c_act, w, b, n):
    m = c_act @ w + b
    return jnp.split(m[:, None, :], n, axis=-1)


def rope_tables(seq):
    inv = ROPE_THETA ** (-jnp.arange(0, ROT_DIM, 2, dtype=jnp.float32) / ROT_DIM)
    ang = jnp.arange(seq, dtype=jnp.float32)[:, None] * inv[None, :]
    return jnp.cos(ang), jnp.sin(ang)


def partial_rope(t, cos, sin):
    half = ROT_DIM // 2
    cos = cos[None, :, None, :].astype(t.dtype)
    sin = sin[None, :, None, :].astype(t.dtype)
    t1 = t[..., :half]
    t2 = t[..., half:ROT_DIM]
    return jnp.concatenate([t1 * cos - t2 * sin, t2 * cos + t1 * sin, t[..., ROT_DIM:]], axis=-1)


def short_conv_mixer(h, w_in, w_conv, w_out):
    b_gate, c_gate, u = jnp.split(h @ w_in, 3, axis=-1)
    z = c_gate * u
    z = lax.conv_general_dilated(
        z, w_conv[:, None, :].astype(z.dtype),
        window_strides=(1,), padding=[(CONV_WIDTH - 1, 0)],
        dimension_numbers=('NWC', 'WIO', 'NWC'), feature_group_count=D_MODEL)
    return (b_gate * z) @ w_out


def squared_relu_mlp(h, w_up, w_down):
    a = jax.nn.relu(h @ w_up)
    return (a * a) @ w_down


def shared_kv(h_res, c_act, kv_ada_w, kv_ada_b, kv_norm, w_kv, b_kv, cos, sin):
    shift, scale = ada_mod(c_act, kv_ada_w, kv_ada_b, 2)
    h = rms_norm(h_res, kv_norm) * (1 + scale) + shift
    b, s, _ = h.shape
    kv = (h @ w_kv + b_kv).reshape(b, s, 2, N_KV_HEADS, HEAD_DIM)
    k = partial_rope(kv[:, :, 0], cos, sin)
    v = kv[:, :, 1]
    return k, v


def band_blocks(t):
    b, s, nh, d = t.shape
    nb = s // BLOCK
    prev = jnp.pad(t, ((0, 0), (BLOCK, 0), (0, 0), (0, 0)))[:, :s]
    return jnp.concatenate([prev.reshape(b, nb, BLOCK, nh, d),
                            t.reshape(b, nb, BLOCK, nh, d)], axis=2)


def sliding_window_sink_attention(h, k, v, w_q, b_q, sinks, w_o, b_o, cos, sin):
    b, s, _ = h.shape
    nb = s // BLOCK
    q = partial_rope((h @ w_q + b_q).reshape(b, s, N_HEADS, HEAD_DIM), cos, sin)
    qb = q.reshape(b, nb, BLOCK, N_KV_HEADS, GROUP, HEAD_DIM)
    kb = band_blocks(k)
    vb = band_blocks(v)
    scores = jnp.einsum('bnqkgd,bnskd->bnkgqs', qb, kb).astype(jnp.float32) * (HEAD_DIM ** -0.5)
    blk = jnp.arange(nb)[:, None, None]
    q_pos = blk * BLOCK + jnp.arange(BLOCK)[None, :, None]
    k_pos = (blk - 1) * BLOCK + jnp.arange(2 * BLOCK)[None, None, :]
    valid = (k_pos <= q_pos) & (k_pos > q_pos - WINDOW) & (k_pos >= 0)
    scores = jnp.where(valid[None, :, None, None], scores, -jnp.inf)
    sink = jnp.broadcast_to(
        sinks.astype(jnp.float32).reshape(N_KV_HEADS, GROUP)[None, None, :, :, None, None],
        scores.shape[:-1] + (1,))
    probs = jax.nn.softmax(jnp.concatenate([scores, sink], axis=-1), axis=-1)[..., :-1]
    o = jnp.einsum('bnkgqs,bnskd->bnqkgd', probs.astype(vb.dtype), vb)
    return o.reshape(b, s, N_HEADS * HEAD_DIM) @ w_o + b_o


def setup_inputs(seed: int = 0) -> dict:
    key = jax.random.key(seed)
    ks = jax.random.split(key, 21)
    D = D_MODEL
    nrm = jax.random.normal
    kv_dim = 2 * N_KV_HEADS * HEAD_DIM
    q_dim = N_HEADS * HEAD_DIM
    return {
        "x": nrm(ks[0], (BATCH, SEQ, D), jnp.float32),
        "c": nrm(ks[1], (BATCH, D), jnp.float32),
        "ada_w": nrm(ks[2], (DEPTH, 2, D, 3 * D), jnp.float32) * (0.5 * D ** -0.5),
        "ada_b": nrm(ks[3], (DEPTH, 2, 3 * D), jnp.float32) * 0.02,
        "norm_pre": 1.0 + 0.1 * nrm(ks[4], (DEPTH, 2, D), jnp.float32),
        "norm_post": 1.0 + 0.1 * nrm(ks[5], (DEPTH, 2, D), jnp.float32),
        "conv_w_in": nrm(ks[6], (N_SELF, D, 3 * D), jnp.float32) * D ** -0.5,
        "conv_w": nrm(ks[7], (N_SELF, CONV_WIDTH, D), jnp.float32) * CONV_WIDTH ** -0.5,
        "conv_w_out": nrm(ks[8], (N_SELF, D, D), jnp.float32) * D ** -0.5,
        "kv_ada_w": nrm(ks[9], (D, 2 * D), jnp.float32) * (0.5 * D ** -0.5),
        "kv_ada_b": nrm(ks[10], (2 * D,), jnp.float32) * 0.02,
        "kv_norm": 1.0 + 0.1 * nrm(ks[11], (D,), jnp.float32),
        "w_kv": nrm(ks[12], (D, kv_dim), jnp.float32) * D ** -0.5,
        "b_kv": nrm(ks[13], (kv_dim,), jnp.float32) * 0.02,
        "w_q": nrm(ks[14], (N_CROSS, D, q_dim), jnp.float32) * D ** -0.5,
        "b_q": nrm(ks[15], (N_CROSS, q_dim), jnp.float32) * 0.02,
        "sinks": nrm(ks[16], (N_CROSS, N_HEADS), jnp.float32),
        "w_o": nrm(ks[17], (N_CROSS, q_dim, D), jnp.float32) * q_dim ** -0.5,
        "b_o": nrm(ks[18], (N_CROSS, D), jnp.float32) * 0.02,
        "mlp_up": nrm(ks[19], (DEPTH, D, D_FF), jnp.float32) * D ** -0.5,
        "mlp_down": nrm(ks[20], (DEPTH, D_FF, D), jnp.float32) * D_FF ** -0.5,
    }


def reference(x, c, ada_w, ada_b, norm_pre, norm_post, conv_w_in, conv_w, conv_w_out,
              kv_ada_w, kv_ada_b, kv_norm, w_kv, b_kv, w_q, b_q, sinks, w_o, b_o,
              mlp_up, mlp_down):
    cos, sin = rope_tables(x.shape[1])
    c_act = jax.nn.silu(c)
    k = v = None
    for l in range(DEPTH):
        shift, scale, gate = ada_mod(c_act, ada_w[l, 0], ada_b[l, 0], 3)
        h = rms_norm(x, norm_pre[l, 0]) * (1 + scale) + shift
        if l < N_SELF:
            y = short_conv_mixer(h, conv_w_in[l], conv_w[l], conv_w_out[l])
        else:
            if l == N_SELF:
                k, v = shared_kv(x, c_act, kv_ada_w, kv_ada_b, kv_norm, w_kv, b_kv, cos, sin)
            a = l - N_SELF
            y = sliding_window_sink_attention(h, k, v, w_q[a], b_q[a], sinks[a],
                                              w_o[a], b_o[a], cos, sin)
        x = x + gate * rms_norm(y, norm_post[l, 0])
        shift, scale, gate = ada_mod(c_act, ada_w[l, 1], ada_b[l, 1], 3)
        h = rms_norm(x, norm_pre[l, 1]) * (1 + scale) + shift
        x = x + gate * rms_norm(squared_relu_mlp(h, mlp_up[l], mlp_down[l]), norm_post[l, 1])
    return x
```

```python
import contextlib
import numpy as np
import concourse.bass as bass
import concourse.mybir as mybir
from concourse.bass_utils import run_bass_kernel_spmd

F32 = mybir.dt.float32
BF16 = mybir.dt.bfloat16
ALU = mybir.AluOpType
AF = mybir.ActivationFunctionType
AX = mybir.AxisListType

D = 2048
KC = 16
DFF = 8192
SEQ = 4096
NB = 4
NH = 32
HD = 64
HALO = 130
TM = 512
TOK = 2048
LTOK = TOK + HALO
EPS = 1e-6
NEG = -30000.0
NSLOT = 4

PERM = np.array(list(range(0, 8)) + list(range(16, 40)) + list(range(8, 16)) + list(range(40, 64)))


def _vec_layout():
    off = {}
    cur = 0

    def add(name, w):
        nonlocal cur
        off[name] = (cur, w)
        cur += w

    add("c", 16)
    for l in range(2):
        for i in range(2):
            add(f"ada_b{l}{i}", 48)
            add(f"gpre{l}{i}", 16)
            add(f"gpost{l}{i}", 16)
    add("kv_ada_b", 32)
    add("kv_norm", 16)
    add("cw0", 16)
    add("cw1", 16)
    add("cw2", 16)
    add("bk", 4)
    add("bq", 16)
    add("bo", 16)
    add("sinks", 32)
    add("flag", 1)
    add("bv", 256)
    return off, cur


VOFF, NV = _vec_layout()


def _col(v):
    return np.ascontiguousarray(np.asarray(v, np.float32).reshape(-1, 128).T)


class Res:
    __slots__ = ("w", "r")

    def __init__(self, init=None):
        self.w = dict(init) if init else {}
        self.r = {}


def _merge(d, t):
    for k, v in t.items():
        if d.get(k, 0) < v:
            d[k] = v


class Eng:
    def __init__(self, nc, es, h, name):
        self.h = h
        self.name = name
        self.sem = es.enter_context(nc.semaphore("e_" + name))
        self.cnt = 0
        self.seen = {}


class Sched:
    def __init__(self, nc, es):
        self.nc = nc
        self.pe = Eng(nc, es, nc.tensor, "pe")
        self.act = Eng(nc, es, nc.scalar, "act")
        self.dve = Eng(nc, es, nc.vector, "dve")
        self.pool = Eng(nc, es, nc.gpsimd, "pool")
        self.sp = Eng(nc, es, nc.sync, "sp")

    def _need(self, reads, writes):
        need = {}
        for r in reads:
            _merge(need, r.w)
        for w in writes:
            _merge(need, w.w)
            _merge(need, w.r)
        return need

    def _waits(self, eng, need):
        for sem, v in need.items():
            if eng is self.pe and sem is eng.sem:
                continue
            if eng.seen.get(sem, 0) >= v:
                continue
            eng.h.wait_ge(sem, v)
            eng.seen[sem] = v

    def _commit(self, tok, reads, writes):
        for r in reads:
            _merge(r.r, tok)
        for w in writes:
            _merge(w.w, tok)
            w.r = {}

    def op(self, eng, fn, reads=(), writes=()):
        self._waits(eng, self._need(reads, writes))
        ins = fn()
        eng.cnt += 1
        ins.then_inc(eng.sem, 1)
        tok = {eng.sem: eng.cnt}
        self._commit(tok, reads, writes)
        return tok

    def group(self, eng, fns, reads=(), writes=()):
        self._waits(eng, self._need(reads, writes))
        ins = None
        for f in fns:
            ins = f()
        eng.cnt += 1
        ins.then_inc(eng.sem, 1)
        tok = {eng.sem: eng.cnt}
        self._commit(tok, reads, writes)
        return tok

    def dma(self, eng, out, in_, sem, semcnt, reads=(), writes=()):
        self._waits(eng, self._need(reads, writes))
        ins = eng.h.dma_start(out=out, in_=in_)
        semcnt[0] += 16
        ins.then_inc(sem, 16)
        tok = {sem: semcnt[0]}
        self._commit(tok, reads, writes)
        return tok


def build_nc(n_tiles=4, do_l1=True, stop=None):
    nc = bass.Bass("TRN2", target_bir_lowering=False)

    def din(name, shape):
        return nc.dram_tensor(name, list(shape), F32, kind="ExternalInput").ap()

    xT = din("xT", [D, LTOK])
    vecs = din("vecs", [128, NV])
    ropeC = din("ropeC", [128, LTOK])
    ropeS = din("ropeS", [128, LTOK])
    cst = din("cst", [128, 128 + 512 + 512])
    ada_w = din("ada_w", [2, 2, D, 3 * D])
    kv_ada_w = din("kv_ada_w", [D, 2 * D])
    w_in = din("w_in", [D, 3 * D])
    w_out = din("w_out", [D, D])
    w_up = din("w_up", [2, D, DFF])
    w_down = din("w_down", [2, DFF, D])
    w_kdup = din("w_kdup", [D, 512])
    w_v = din("w_v", [D, 256])
    w_q = din("w_q", [D, D])
    w_o = din("w_o", [D, D])
    outT = nc.dram_tensor("outT", [D, max(1, n_tiles) * TM], F32, kind="ExternalOutput").ap()

    wsrc = {"w_in": w_in, "w_out": w_out, "up0": w_up[0], "down0": w_down[0], "wk": w_kdup, "wv": w_v,
            "wq": w_q, "wo": w_o, "up1": w_up[1], "down1": w_down[1]}
    worder = ["w_in", "w_out", "up0", "down0", "wk", "wv", "wq", "wo", "up1", "down1"]
    wscr = {k: nc.dram_tensor("scr_" + k, list(v.shape), BF16, kind="Internal").ap() for k, v in wsrc.items()}

    es = contextlib.ExitStack()
    with es:
        sc = Sched(nc, es)
        PE, ACT, DVE, POOL, SP = sc.pe, sc.act, sc.dve, sc.pool, sc.sp

        def sb(name, shape, dt):
            return es.enter_context(nc.sbuf_tensor(name, list(shape), dt))

        X = sb("X", [128, KC * TM], F32)
        XA = sb("XA", [128, 2 * KC * TM], BF16)
        BIG = sb("BIG", [128, 64 * TM + 128], BF16)
        WR = [sb(f"wr{i}", [128, 16, 256], BF16) for i in range(NSLOT)]
        KT = sb("KT", [128, 4, 2, 640], BF16)
        V = sb("V", [128, 5, 256], BF16)
        CT = sb("CT", [128, TM], F32)
        ST = sb("ST", [128, TM], F32)
        VEC = sb("VEC", [128, NV], F32)
        MOD = sb("MOD", [128, 4 * 48 + 32], F32)
        DER = sb("DER", [128, 16 * 16], F32)
        IDN = sb("IDN", [128, 128], BF16)
        MSK = sb("MSK", [128, 1024], BF16)
        ONES = sb("ONES", [128, 128], BF16)
        CACT = sb("CACT", [128, 16], BF16)
        RSTD = sb("RSTD", [128, TM], F32)
        TMP = [sb(f"tmp{i}", [128, TM], F32) for i in range(2)]
        TB = [sb(f"tb{i}", [128, TM], F32) for i in range(2)]
        SQ = [sb(f"sq{i}", [128, TM], BF16) for i in range(3)]
        ZT = sb("ZT", [128, 16, 2], F32)
        PF = [sb(f"pf{i}", [128, 512], F32) for i in range(2)]
        PN = [sb(f"pn{i}", [128, 512], BF16) for i in range(2)]
        PTS = [sb(f"pts{i}", [128, 512], BF16) for i in range(2)]
        SM = [sb(f"sm{i}", [128, 16], F32) for i in range(2)]
        SINK8 = sb("SINK8", [128, 32], F32)

        PS = [es.enter_context(nc.psum_tensor(f"ps{i}", [128, 512], F32)) for i in range(8)]

        rX = [Res() for _ in range(KC)]
        rXA = [Res() for _ in range(32)]
        rBIG = [Res() for _ in range(64)]
        rWR = [Res() for _ in range(NSLOT)]
        rPS = [Res() for _ in range(8)]
        rPTP = [Res(), Res()]
        rKT = Res()
        rV = [Res() for _ in range(5)]
        rCT = Res()
        rVEC = Res()
        rMOD = Res()
        rDER = Res()
        rCST = Res()
        rCACT = Res()
        rRSTD = Res()
        rTMP = [Res() for _ in range(3)]
        rTB = [Res() for _ in range(2)]
        rSQ = [Res() for _ in range(3)]
        rZT = Res()
        rPF = [Res(), Res()]
        rPN = [Res(), Res()]
        rPTS = [Res(), Res()]
        rSM = [Res(), Res()]
        rSINK = Res()

        wsem = [es.enter_context(nc.semaphore(f"wsem{i}")) for i in range(NSLOT)]
        wcnt = [[0] for _ in range(NSLOT)]
        whsem = [es.enter_context(nc.semaphore(f"whsem{i}")) for i in range(NSLOT)]
        whcnt = [[0] for _ in range(NSLOT)]
        pcsem = {k: es.enter_context(nc.semaphore("pc_" + k)) for k in worder}
        pccnt = {k: [0] for k in worder}
        rSCR = {k: Res() for k in worder}
        pcstate = {"n": 0, "look": 2, "tix": 0}

        def precast_upto(idx):
            while pcstate["n"] <= min(idx, len(worder) - 1):
                k = worder[pcstate["n"]]
                pcstate["n"] += 1
                src, dst = wsrc[k], wscr[k]
                rows = src.shape[0]
                step = 256
                for r0 in range(0, rows, step):
                    sc.dma(POOL, dst[r0:r0 + step, :], src[r0:r0 + step, :], pcsem[k], pccnt[k], writes=[rSCR[k]])

        csem = es.enter_context(nc.semaphore("csem"))
        ccnt = [0]
        csem2 = es.enter_context(nc.semaphore("csem2"))
        ccnt2 = [0]
        xsem = es.enter_context(nc.semaphore("xsem"))
        xcnt = [0]
        tsem = es.enter_context(nc.semaphore("tsem"))
        tcnt = [0]
        osem = es.enter_context(nc.semaphore("osem"))
        ocnt = [0]

        def vcol(name, j=0, w=1):
            o, _ = VOFF[name]
            return VEC[:, o + j:o + j + w]

        def x_ch(c, T):
            return X[:, c * T:(c + 1) * T]

        def xa_ch(which, c, T):
            base = which * KC * TM
            return XA[:, base + c * T: base + (c + 1) * T]

        def rxa(which, c):
            return rXA[which * 16 + c]

        def y_xa(c, T):
            v = XA[:, :].bitcast(F32)
            return v[:, c * T:(c + 1) * T]

        def ry_xa(c):
            return [rXA[2 * c], rXA[2 * c + 1]]

        def big_bf(g, T):
            return BIG[:, g * T:(g + 1) * T]

        def big_f32(j, T, base_g=0, extra=0):
            v = BIG[:, :].bitcast(F32)
            o = base_g * TM // 2
            return v[:, o + j * (T + extra): o + (j + 1) * (T + extra)]

        def rbig_f32(j, base_g=0):
            return [rBIG[base_g + 2 * j], rBIG[base_g + 2 * j + 1]]

        sc.dma(SP, VEC[:, :], vecs, csem, ccnt, writes=[rVEC])
        sc.dma(POOL, IDN[:, :], cst[:, 0:128], csem2, ccnt2, writes=[rCST])
        sc.dma(POOL, MSK[:, :], cst[:, 128:1152], csem2, ccnt2, writes=[rCST])
        sc.op(DVE, lambda: nc.vector.memset(ONES[:, :], 1.0), writes=[rCST])
        sc.op(DVE, lambda: nc.vector.memset(ZT[:, :, :], 0.0), writes=[rZT])
        sc.op(DVE, lambda: nc.vector.memset(KT[:, :, :, :], 0.0), writes=[rKT])
        sc.op(ACT, lambda: nc.scalar.activation(CACT[:, :], vcol("c", 0, 16), AF.Silu), reads=[rVEC], writes=[rCACT])
        sc.op(DVE, lambda: nc.vector.tensor_scalar(SINK8[:, :], vcol("sinks", 0, 32), 8.0, None, ALU.mult),
              reads=[rVEC], writes=[rSINK])

        wstate = {"n": 0}

        def load_slab(src, wkey=None):
            s = wstate["n"] % NSLOT
            wstate["n"] += 1
            kc, ncol = src.shape[1], src.shape[2]
            if wkey is None:
                sc.dma(POOL, WR[s][:, 0:kc, 0:ncol], src, wsem[s], wcnt[s], writes=[rWR[s]])
            else:
                sc.dma(SP, WR[s][:, 0:kc, 0:ncol], src, whsem[s], whcnt[s], reads=[rSCR[wkey]], writes=[rWR[s]])
            return WR[s], rWR[s]

        pstate = {"n": 0}

        def next_pset():
            s = pstate["n"] % 3
            pstate["n"] += 1
            return (2 * s, 2 * s + 1)

        L1KEYS = ("wk", "wv", "wq", "wo", "up1", "down1")

        def wsel(key):
            if key in L1KEYS and pcstate["tix"] <= 1:
                return wsrc[key], None
            if pcstate["tix"] <= 1:
                precast_upto(3)
            else:
                precast_upto(len(worder) - 1)
            return wscr[key], key

        def proj_fm(W, krows, ncols, in_aps, in_res, T, epi):
            wkey = None
            if isinstance(W, str):
                W, wkey = wsel(W)
            KS = krows // 2048
            deferred = []
            for cg in range(ncols // 256):
                pset = next_pset()
                for ks in range(KS):
                    src = W[ks * 2048:(ks + 1) * 2048, cg * 256:(cg + 1) * 256].rearrange("(k p) c -> p k c", p=128)
                    slab, sres = load_slab(src, wkey)
                    fns = []
                    for j in range(2):
                        for ic in range(16):
                            kidx = ks * 16 + ic
                            fns.append(lambda j=j, ic=ic, kidx=kidx, slab=slab, pset=pset, ks=ks:
                                       nc.tensor.matmul(PS[pset[j]][:, 0:T], slab[:, ic, j * 128:(j + 1) * 128],
                                                        in_aps[kidx], start=(ks == 0 and ic == 0),
                                                        stop=(ks == KS - 1 and ic == 15)))
                    sc.group(PE, fns, reads=[sres] + in_res[ks * 16:(ks + 1) * 16],
                             writes=[rPS[pset[0]], rPS[pset[1]]])
                for d in deferred:
                    d()
                deferred = []
                for j in range(2):
                    r = epi(cg * 2 + j, PS[pset[j]][:, 0:T], rPS[pset[j]])
                    if r:
                        deferred.extend(r)
            for d in deferred:
                d()

        sqi = {"n": 0}

        def stats_mm(sq_ap, sq_res, T, first, last):
            sc.group(PE, [lambda: nc.tensor.matmul(PS[6][:, 0:T], ONES[:, :], sq_ap, start=first, stop=last)],
                     reads=[sq_res, rCST], writes=[rPS[6]])

        def finish_rstd(T):
            sc.op(ACT, lambda: nc.scalar.activation(RSTD[:, 0:T], PS[6][:, 0:T], AF.Sqrt, bias=EPS, scale=1.0 / D),
                  reads=[rPS[6]], writes=[rRSTD])
            sc.op(DVE, lambda: nc.vector.reciprocal(RSTD[:, 0:T], RSTD[:, 0:T]), reads=[rRSTD], writes=[rRSTD])

        def prenorm_stats(T):
            for c in range(KC):
                i = sqi["n"] % 3
                sqi["n"] += 1
                if c % 2 == 0:
                    sc.op(ACT, lambda c=c, i=i: nc.scalar.activation(SQ[i][:, 0:T], x_ch(c, T), AF.Square),
                          reads=[rX[c]], writes=[rSQ[i]])
                else:
                    sc.op(DVE, lambda c=c, i=i: nc.vector.tensor_tensor(SQ[i][:, 0:T], x_ch(c, T), x_ch(c, T), ALU.mult),
                          reads=[rX[c]], writes=[rSQ[i]])
                stats_mm(SQ[i][:, 0:T], rSQ[i], T, c == 0, c == KC - 1)
            finish_rstd(T)

        tmi = {"n": 0}

        def modulate(which, T, acol, bcol):
            for c in range(KC):
                i = tmi["n"] % 2
                tmi["n"] += 1
                sc.op(DVE, lambda c=c, i=i: nc.vector.scalar_tensor_tensor(TMP[i][:, 0:T], x_ch(c, T), acol(c), RSTD[:, 0:T],
                                                                          ALU.mult, ALU.mult),
                      reads=[rX[c], rRSTD, rDER, rMOD], writes=[rTMP[i]])
                sc.op(ACT, lambda c=c, i=i: nc.scalar.activation(xa_ch(which, c, T), TMP[i][:, 0:T], AF.Identity,
                                                                bias=bcol(c), scale=1.0),
                      reads=[rTMP[i], rMOD, rDER], writes=[rxa(which, c)])

        def post_epi_factory(ydst, rydst, T, biascol=None):
            def epi(oc, ps, pres):
                if biascol is None:
                    sc.op(ACT, lambda: nc.scalar.copy(ydst(oc), ps), reads=[pres], writes=rydst(oc))
                else:
                    sc.op(ACT, lambda: nc.scalar.activation(ydst(oc), ps, AF.Identity, bias=biascol(oc), scale=1.0),
                          reads=[pres, rVEC], writes=rydst(oc))
                i = sqi["n"] % 3
                sqi["n"] += 1
                sc.op(DVE, lambda: nc.vector.tensor_tensor(SQ[i][:, 0:T], ydst(oc), ydst(oc), ALU.mult),
                      reads=rydst(oc), writes=[rSQ[i]])
                return [lambda: stats_mm(SQ[i][:, 0:T], rSQ[i], T, oc == 0, oc == KC - 1)]
            return epi

        def postnorm_apply(ysrc, rysrc, T, gcol):
            finish_rstd(T)
            for c in range(KC):
                i = tmi["n"] % 2
                tmi["n"] += 1
                sc.op(DVE, lambda c=c, i=i: nc.vector.scalar_tensor_tensor(TMP[i][:, 0:T], ysrc(c), gcol(c), RSTD[:, 0:T],
                                                                          ALU.mult, ALU.mult),
                      reads=rysrc(c) + [rRSTD, rDER], writes=[rTMP[i]])
                sc.op(DVE, lambda c=c, i=i: nc.vector.tensor_tensor(x_ch(c, T), x_ch(c, T), TMP[i][:, 0:T], ALU.add),
                      reads=[rTMP[i]], writes=[rX[c]])

        cact_aps = [CACT[:, c:c + 1] for c in range(KC)]
        cact_res = [rCACT] * KC

        def ada_stage(W, ncols, modoff, bname):
            def epi(oc, ps, pres):
                sc.op(DVE, lambda: nc.vector.tensor_tensor(MOD[:, modoff + oc:modoff + oc + 1], ps, vcol(bname, oc, 1), ALU.add),
                      reads=[pres, rVEC], writes=[rMOD])
                return None
            proj_fm(W, 2048, ncols, cact_aps, cact_res, 1, epi)

        for l in range(2):
            for i in range(2):
                ada_stage(ada_w[l, i], 3 * D, (l * 2 + i) * 48, f"ada_b{l}{i}")
                if l == 0 and i == 0:
                    precast_upto(3)
        ada_stage(kv_ada_w, 2 * D, 192, "kv_ada_b")

        def der(j, w=16):
            return DER[:, j * 16:j * 16 + w]

        for l in range(2):
            for i in range(2):
                k = l * 2 + i
                mo = k * 48
                sc.op(DVE, lambda: nc.vector.tensor_scalar(der(2 * k), MOD[:, mo + 16:mo + 32], 1.0, None, ALU.add),
                      reads=[rMOD], writes=[rDER])
                sc.op(DVE, lambda: nc.vector.tensor_tensor(der(2 * k), der(2 * k), vcol(f"gpre{l}{i}", 0, 16), ALU.mult),
                      reads=[rDER, rVEC], writes=[rDER])
                sc.op(DVE, lambda: nc.vector.tensor_tensor(der(2 * k + 1), MOD[:, mo + 32:mo + 48], vcol(f"gpost{l}{i}", 0, 16), ALU.mult),
                      reads=[rMOD, rVEC], writes=[rDER])
        sc.op(DVE, lambda: nc.vector.tensor_scalar(der(8), MOD[:, 192 + 16:192 + 32], 1.0, None, ALU.add),
              reads=[rMOD], writes=[rDER])
        sc.op(DVE, lambda: nc.vector.tensor_tensor(der(8), der(8), vcol("kv_norm", 0, 16), ALU.mult),
              reads=[rDER, rVEC], writes=[rDER])

        if stop == "ada":
            sc.dma(POOL, outT[0:128, 0:224], MOD[:, :], osem, ocnt, reads=[rMOD, rDER])
            nc.gpsimd.wait_ge(osem, ocnt[0])
            return nc

        def Acol(l, i):
            return lambda c: DER[:, (2 * (l * 2 + i)) * 16 + c:(2 * (l * 2 + i)) * 16 + c + 1]

        def Gcol(l, i):
            return lambda c: DER[:, (2 * (l * 2 + i) + 1) * 16 + c:(2 * (l * 2 + i) + 1) * 16 + c + 1]

        def Bcol(l, i):
            return lambda c: MOD[:, (l * 2 + i) * 48 + c:(l * 2 + i) * 48 + c + 1]

        def Akv(c):
            return DER[:, 128 + c:128 + c + 1]

        def Bkv(c):
            return MOD[:, 192 + c:192 + c + 1]

        def mlp(l, T):
            prenorm_stats(T)
            modulate(0, T, Acol(l, 1), Bcol(l, 1))

            def up_epi(oc, ps, pres):
                i = tmi["n"] % 2
                tmi["n"] += 1
                sc.op(ACT, lambda: nc.scalar.activation(TMP[i][:, 0:T], ps, AF.Relu), reads=[pres], writes=[rTMP[i]])
                sc.op(DVE, lambda: nc.vector.tensor_tensor(big_bf(oc, T), TMP[i][:, 0:T], TMP[i][:, 0:T], ALU.mult),
                      reads=[rTMP[i]], writes=[rBIG[oc]])
                return None
            proj_fm(f"up{l}", 2048, DFF, [xa_ch(0, c, T) for c in range(KC)], [rxa(0, c) for c in range(KC)], T, up_epi)
            proj_fm(f"down{l}", DFF, D, [big_bf(g, T) for g in range(64)], [rBIG[g] for g in range(64)], T,
                    post_epi_factory(lambda oc: y_xa(oc, T), ry_xa, T))
            postnorm_apply(lambda c: y_xa(c, T), ry_xa, T, Gcol(l, 1))

        def rope_chunk(src, rsrc, dst, rdst, T, c0, n):
            i = tmi["n"] % 2
            tmi["n"] += 1
            t2 = TMP[i]
            for q in range(4):
                qs = q ^ 1
                sc.op(DVE, lambda q=q, qs=qs: nc.vector.tensor_tensor(t2[q * 32:(q + 1) * 32, 0:n], src[qs * 32:(qs + 1) * 32, c0:c0 + n],
                                                                     ST[qs * 32:(qs + 1) * 32, c0:c0 + n], ALU.mult),
                      reads=[rsrc, rCT], writes=[rTMP[i]])
            sc.op(DVE, lambda: nc.vector.tensor_tensor(src[:, c0:c0 + n], src[:, c0:c0 + n], CT[:, c0:c0 + n], ALU.mult),
                  reads=[rsrc, rCT], writes=[rsrc])
            dl = dst if isinstance(dst, list) else [((0, 128), dst)]
            for (r0, r1), dap in dl:
                sc.op(DVE, lambda r0=r0, r1=r1, dap=dap: nc.vector.tensor_tensor(dap, src[r0:r1, c0:c0 + n], t2[r0:r1, 0:n], ALU.add),
                      reads=[rsrc, rTMP[i]], writes=rdst)

        tiles = [("halo", 0, HALO)] + [("main", HALO + j * TM, TM) for j in range(n_tiles)]
        tbi = {"n": 0}
        for tix, (kind, t0, T) in enumerate(tiles):
            is_halo = kind == "halo"
            pcstate["tix"] = tix
            if tix == 2:
                precast_upto(len(worder) - 1)
            sc.dma(POOL, X[:, 0:KC * T].rearrange("p (c t) -> p c t", t=T),
                   xT[:, t0:t0 + T].rearrange("(c p) t -> p c t", p=128), xsem, xcnt, writes=rX)
            sc.dma(POOL, CT[:, 0:T], ropeC[:, t0:t0 + T], tsem, tcnt, writes=[rCT])
            sc.dma(POOL, ST[:, 0:T], ropeS[:, t0:t0 + T], tsem, tcnt, writes=[rCT])

            prenorm_stats(T)
            modulate(0, T, Acol(0, 0), Bcol(0, 0))
            ZW = T + 2

            def zch(i):
                return big_f32(i, T, base_g=32, extra=2)

            def rz(i):
                return [rBIG[32 + 2 * i], rBIG[32 + 2 * i + 1], rBIG[min(63, 32 + 2 * i + 2)]]

            zall = BIG[:, :].bitcast(F32)[:, 16 * TM:16 * TM + 16 * ZW].rearrange("p (c t) -> p c t", t=ZW)
            rzall = [rBIG[g] for g in range(32, 64)]
            sc.op(DVE, lambda: nc.vector.tensor_copy(zall[:, :, 0:2], ZT[:, :, :]), reads=[rZT], writes=rzall)

            def conv_chunk(i):
                k = tmi["n"] % 2
                tmi["n"] += 1
                z = zch(i)
                sc.op(ACT, lambda: nc.scalar.activation(TMP[k][:, 0:T], z[:, 2:ZW], AF.Copy, scale=vcol("cw2", i, 1)),
                      reads=rz(i) + [rVEC], writes=[rTMP[k]])
                sc.op(DVE, lambda: nc.vector.scalar_tensor_tensor(TMP[k][:, 0:T], z[:, 1:ZW - 1], vcol("cw1", i, 1), TMP[k][:, 0:T],
                                                                 ALU.mult, ALU.add),
                      reads=rz(i) + [rVEC, rTMP[k]], writes=[rTMP[k]])
                sc.op(DVE, lambda: nc.vector.scalar_tensor_tensor(TMP[k][:, 0:T], z[:, 0:ZW - 2], vcol("cw0", i, 1), TMP[k][:, 0:T],
                                                                 ALU.mult, ALU.add),
                      reads=rz(i) + [rVEC, rTMP[k]], writes=[rTMP[k]])
                sc.op(DVE, lambda: nc.vector.tensor_tensor(xa_ch(1, i, T), TMP[k][:, 0:T], big_f32(i, T), ALU.mult),
                      reads=[rTMP[k]] + rbig_f32(i), writes=[rxa(1, i)])

            def win_epi(oc, ps, pres):
                if oc < 16:
                    sc.op(ACT, lambda: nc.scalar.copy(big_f32(oc, T), ps), reads=[pres], writes=rbig_f32(oc))
                elif oc < 32:
                    i = oc - 16
                    sc.op(ACT, lambda: nc.scalar.copy(zch(i)[:, 2:ZW], ps), reads=[pres], writes=rz(i))
                else:
                    i = oc - 32
                    sc.op(DVE, lambda: nc.vector.tensor_tensor(zch(i)[:, 2:ZW], zch(i)[:, 2:ZW], ps, ALU.mult),
                          reads=[pres] + rz(i), writes=rz(i))
                    conv_chunk(i)
                return None
            proj_fm("w_in", 2048, 3 * D, [xa_ch(0, c, T) for c in range(KC)], [rxa(0, c) for c in range(KC)], T, win_epi)

            if is_halo:
                sc.op(DVE, lambda: nc.vector.tensor_scalar(ZT[:, :, :], zall[:, :, T:T + 2], vcol("flag", 0, 1), None, ALU.mult),
                      reads=rzall + [rVEC], writes=[rZT])
            else:
                sc.op(DVE, lambda: nc.vector.tensor_copy(ZT[:, :, :], zall[:, :, T:T + 2]), reads=rzall, writes=[rZT])

            proj_fm("w_out", 2048, D, [xa_ch(1, c, T) for c in range(KC)], [rxa(1, c) for c in range(KC)], T,
                    post_epi_factory(lambda oc: big_f32(oc, T), rbig_f32, T))
            postnorm_apply(lambda c: big_f32(c, T), rbig_f32, T, Gcol(0, 0))

            mlp(0, T)

            if not do_l1:
                if is_halo and n_tiles == 0:
                    sc.dma(POOL, outT[:, 0:T].rearrange("(c p) t -> p c t", p=128),
                           X[:, 0:KC * T].rearrange("p (c t) -> p c t", t=T), osem, ocnt, reads=rX)
                if not is_halo:
                    j = tix - 1
                    sc.dma(POOL, outT[:, j * TM:(j + 1) * TM].rearrange("(c p) t -> p c t", p=128),
                           X[:, 0:KC * T].rearrange("p (c t) -> p c t", t=T), osem, ocnt, reads=rX)
                continue

            prenorm_stats(T)
            modulate(0, T, Akv, Bkv)
            if not is_halo:
                modulate(1, T, Acol(1, 0), Bcol(1, 0))

            def k_epi(g, ps, pres):
                i = tbi["n"] % 2
                tbi["n"] += 1
                sc.op(ACT, lambda: nc.scalar.activation(TB[i][:, 0:T], ps, AF.Identity, bias=vcol("bk", g, 1), scale=1.0),
                      reads=[pres, rVEC], writes=[rTB[i]])
                if is_halo:
                    rope_chunk(TB[i], rTB[i], [((0, 64), KT[0:64, g, 0, 0:128]), ((64, 128), KT[64:128, g, 1, 0:128])], [rKT], T, 2, 128)
                else:
                    rope_chunk(TB[i], rTB[i], [((0, 64), KT[0:64, g, 0, 128:640]), ((64, 128), KT[64:128, g, 1, 128:640])], [rKT], T, 0, T)
                return None
            proj_fm("wk", 2048, 512, [xa_ch(0, c, T) for c in range(KC)], [rxa(0, c) for c in range(KC)], T, k_epi)

            wv_ap, wv_key = wsel("wv")
            vslab, vres = load_slab(wv_ap.rearrange("(k p) c -> p k c", p=128), wv_key)
            nblk = 1 if is_halo else 4
            for tb in range(nblk):
                c0 = 2 if is_halo else tb * 128
                slot = 0 if is_halo else 1 + tb
                pb = 4 + (tb % 2)
                fns = [lambda ic=ic, c0=c0, pb=pb: nc.tensor.matmul(PS[pb][:, 0:256], xa_ch(0, ic, T)[:, c0:c0 + 128], vslab[:, ic, :],
                                                                   start=(ic == 0), stop=(ic == 15)) for ic in range(KC)]
                sc.group(PE, fns, reads=[vres] + [rxa(0, c) for c in range(KC)], writes=[rPS[pb]])
                sc.op(DVE, lambda slot=slot, pb=pb: nc.vector.tensor_tensor(V[:, slot, :], PS[pb][:, 0:256], vcol("bv", 0, 256), ALU.add),
                      reads=[rPS[pb], rVEC], writes=[rV[slot]])
            if is_halo:
                if n_tiles == 0:
                    sc.dma(POOL, outT[:, 0:T].rearrange("(c p) t -> p c t", p=128),
                           X[:, 0:KC * T].rearrange("p (c t) -> p c t", t=T), osem, ocnt, reads=rX)
                continue

            def q_epi(oc, ps, pres):
                i = tbi["n"] % 2
                tbi["n"] += 1
                sc.op(ACT, lambda: nc.scalar.activation(TB[i][:, 0:T], ps, AF.Identity, bias=vcol("bq", oc, 1), scale=1.0),
                      reads=[pres, rVEC], writes=[rTB[i]])
                rope_chunk(TB[i], rTB[i], big_bf(oc, T), [rBIG[oc]], T, 0, T)
                return None
            proj_fm("wq", 2048, D, [xa_ch(1, c, T) for c in range(KC)], [rxa(1, c) for c in range(KC)], T, q_epi)

            if stop == "q":
                sc.dma(POOL, outT[:, 0:T].rearrange("(c p) t -> p c t", p=128),
                       X[:, 0:KC * T].rearrange("p (c t) -> p c t", t=T), osem, ocnt, reads=rX + [rBIG[g_] for g_ in range(16)])
                nc.gpsimd.wait_ge(osem, ocnt[0])
                return nc
            def emit_S1(c, qb, a):
                g = c // 4
                sb_ = a
                first_blk = (tix == 1 and qb == 0)
                moff = 512 if first_blk else 0
                q0 = qb * 128
                fns = [
                    lambda: nc.tensor.matmul(PS[sb_][:, 0:512], IDN[:, :], MSK[:, moff:moff + 512], start=True, stop=False),
                    lambda: nc.tensor.matmul(PS[sb_][:, 0:256], big_bf(c, T)[:, q0:q0 + 128],
                                             KT[:, g, 0, q0:q0 + 256], start=False, stop=False),
                    lambda: nc.tensor.matmul(PS[sb_][:, 256:512], big_bf(c, T)[:, q0:q0 + 128],
                                             KT[:, g, 1, q0:q0 + 256], start=False, stop=True),
                ]
                sc.group(PE, fns, reads=[rCST, rBIG[c], rKT], writes=[rPS[sb_]])
                sm = SM[a]
                sc.op(DVE, lambda: nc.vector.reduce_max(sm[:, 0:2], PS[sb_][:, 0:512].rearrange("p (h k) -> p h k", h=2), AX.X),
                      reads=[rPS[sb_]], writes=[rSM[a]])
                sc.op(DVE, lambda: nc.vector.tensor_tensor(sm[:, 0:2], sm[:, 0:2], SINK8[:, 2 * c:2 * c + 2], ALU.max),
                      reads=[rSM[a], rSINK], writes=[rSM[a]])
                sc.op(DVE, lambda: nc.vector.tensor_scalar(sm[:, 2:4], sm[:, 0:2], -0.125, None, ALU.mult),
                      reads=[rSM[a]], writes=[rSM[a]])
                sc.op(DVE, lambda: nc.vector.memset(sm[:, 4:6], 0.0), writes=[rSM[a]])
                for hh in range(2):
                    sc.op(ACT, lambda hh=hh: nc.scalar.activation(PF[a][:, hh * 256:(hh + 1) * 256], PS[sb_][:, hh * 256:(hh + 1) * 256],
                                                                  AF.Exp, bias=sm[:, 2 + hh:3 + hh], scale=0.125,
                                                                  accum_out=sm[:, 4 + hh:5 + hh]),
                          reads=[rPS[sb_], rSM[a]], writes=[rPF[a], rSM[a]])
                    sc.op(ACT, lambda hh=hh: nc.scalar.activation(sm[:, 6 + hh:7 + hh], vcol("sinks", 2 * c + hh, 1), AF.Exp,
                                                                  bias=sm[:, 2 + hh:3 + hh], scale=1.0),
                          reads=[rSM[a], rVEC], writes=[rSM[a]])

            def emit_S2(c, qb, a):
                sm = SM[a]
                sc.op(DVE, lambda: nc.vector.tensor_tensor(sm[:, 8:10], sm[:, 4:6], sm[:, 6:8], ALU.add),
                      reads=[rSM[a]], writes=[rSM[a]])
                sc.op(DVE, lambda: nc.vector.reciprocal(sm[:, 8:10], sm[:, 8:10]), reads=[rSM[a]], writes=[rSM[a]])
                for hh in range(2):
                    sc.op(DVE, lambda hh=hh: nc.vector.tensor_scalar(PN[a][:, hh * 256:(hh + 1) * 256], PF[a][:, hh * 256:(hh + 1) * 256],
                                                                     sm[:, 8 + hh:9 + hh], None, ALU.mult),
                          reads=[rPF[a], rSM[a]], writes=[rPN[a]])

            def emit_T(c, qb, a):
                fns = [lambda k=k: nc.tensor.matmul(PS[4 + a][:, k * 128:(k + 1) * 128], PN[a][:, k * 128:(k + 1) * 128], IDN[:, :],
                                                    start=True, stop=True)
                       for k in range(4)]
                sc.group(PE, fns, reads=[rPN[a], rCST], writes=[rPS[4 + a]])
                sc.op(ACT, lambda: nc.scalar.copy(PTS[a][:, :], PS[4 + a][:, 0:512]), reads=[rPS[4 + a]], writes=[rPTS[a]])

            def emit_PV(c, qb, a):
                g = c // 4
                q0 = qb * 128
                ob = (2, 3) if c % 2 == 0 else (6, 7)
                fns = []
                for hh in range(2):
                    for kb in range(2):
                        vs = qb + kb
                        fns.append(lambda hh=hh, kb=kb, vs=vs:
                                   nc.tensor.matmul(PS[ob[hh]][0:64, q0:q0 + 128], V[:, vs, g * 64:(g + 1) * 64],
                                                    PTS[a][:, (hh * 2 + kb) * 128:(hh * 2 + kb + 1) * 128],
                                                    start=(kb == 0), stop=(kb == 1)))
                sc.group(PE, fns, reads=[rPTS[a], rV[qb], rV[qb + 1]], writes=[rPS[ob[0]], rPS[ob[1]]])
                if qb == 3:
                    sc.op(ACT, lambda: nc.scalar.copy(big_bf(16 + c, T)[0:64, :], PS[ob[0]][0:64, 0:T]), reads=[rPS[ob[0]]], writes=[rBIG[16 + c]])
                    sc.op(ACT, lambda: nc.scalar.copy(big_bf(16 + c, T)[64:128, :], PS[ob[1]][0:64, 0:T]), reads=[rPS[ob[1]]], writes=[rBIG[16 + c]])

            pairs = [(c, qb, (c * 4 + qb) % 2) for c in range(KC) for qb in range(4)]
            NP_ = len(pairs)
            for pi in range(NP_ + 3):
                if pi < NP_:
                    emit_S1(*pairs[pi])
                if 1 <= pi <= NP_:
                    emit_S2(*pairs[pi - 1])
                if 2 <= pi <= NP_ + 1:
                    emit_T(*pairs[pi - 2])
                if pi >= 3:
                    emit_PV(*pairs[pi - 3])

            if stop == "att":
                sc.dma(POOL, outT[:, 0:T].rearrange("(c p) t -> p c t", p=128),
                       X[:, 0:KC * T].rearrange("p (c t) -> p c t", t=T), osem, ocnt, reads=rX + [rBIG[g_] for g_ in range(32)])
                nc.gpsimd.wait_ge(osem, ocnt[0])
                return nc
            sc.op(DVE, lambda: nc.vector.tensor_copy(KT[:, :, :, 0:128], KT[:, :, :, 512:640]), reads=[rKT], writes=[rKT])
            sc.op(DVE, lambda: nc.vector.tensor_copy(V[:, 0, :], V[:, 4, :]), reads=[rV[4]], writes=[rV[0]])

            proj_fm("wo", 2048, D, [big_bf(16 + c, T) for c in range(KC)], [rBIG[16 + c] for c in range(KC)], T,
                    post_epi_factory(lambda oc: big_f32(oc, T, base_g=32), lambda oc: rbig_f32(oc, 32), T,
                                     biascol=lambda oc: vcol("bo", oc, 1)))
            postnorm_apply(lambda c: big_f32(c, T, base_g=32), lambda c: rbig_f32(c, 32), T, Gcol(1, 0))

            mlp(1, T)

            j = tix - 1
            sc.dma(POOL, outT[:, j * TM:(j + 1) * TM].rearrange("(c p) t -> p c t", p=128),
                   X[:, 0:KC * T].rearrange("p (c t) -> p c t", t=T), osem, ocnt, reads=rX)

        nc.gpsimd.wait_ge(osem, ocnt[0])
    return nc


def _rope_tables(p0):
    inv = (np.float32(500000.0) ** (-np.arange(0, 16, 2, dtype=np.float32) / np.float32(16))).astype(np.float32)
    pos = (np.arange(LTOK, dtype=np.float32) + np.float32(p0 - HALO)).astype(np.float32)
    ang = (pos[:, None] * inv[None, :]).astype(np.float32)
    cos = np.cos(ang).astype(np.float32)
    sin = np.sin(ang).astype(np.float32)
    C = np.ones((128, LTOK), np.float32)
    Sg = np.zeros((128, LTOK), np.float32)
    for r in range(128):
        j = r % 64
        q = j // 32
        jj = j % 32
        if jj < 8:
            C[r] = cos[:, jj]
            Sg[r] = -sin[:, jj] if q == 0 else sin[:, jj]
    Ssw = np.ascontiguousarray(Sg.reshape(2, 2, 32, LTOK)[:, ::-1].reshape(128, LTOK))
    return C, Ssw


def _masks(first_core_half):
    i = np.arange(128)[:, None]
    j = np.arange(128)[None, :]
    mp = np.where(j > i, 0.0, NEG).astype(np.float32)
    mc = np.where(j <= i, 0.0, NEG).astype(np.float32)
    m2 = np.concatenate([mp, mc, mp, mc], axis=1)
    if first_core_half:
        mp0 = np.full((128, 128), NEG, np.float32)
    else:
        mp0 = mp
    m0 = np.concatenate([mp0, mc, mp0, mc], axis=1)
    return m2, m0


def _prep(inputs, n_cores=8):
    f = lambda a: np.ascontiguousarray(np.asarray(a, dtype=np.float32))
    x = f(inputs["x"])
    c = f(inputs["c"])
    ada_w = f(inputs["ada_w"])
    ada_b = f(inputs["ada_b"])
    norm_pre = f(inputs["norm_pre"])
    norm_post = f(inputs["norm_post"])
    conv_w = f(inputs["conv_w"])[0]
    w_kv = f(inputs["w_kv"])
    b_kv = f(inputs["b_kv"])
    w_q = f(inputs["w_q"])[0]
    b_q = f(inputs["b_q"])[0]
    sinks = f(inputs["sinks"])[0]
    b_o = f(inputs["b_o"])[0]

    qcols = np.concatenate([h * 64 + PERM for h in range(NH)])
    w_q_p = np.ascontiguousarray(w_q[:, qcols])
    b_q_p = b_q[qcols]
    kcols = np.concatenate([np.concatenate([g * 64 + PERM, g * 64 + PERM]) for g in range(4)])
    w_kdup = np.ascontiguousarray(w_kv[:, kcols])
    b_kdup = b_kv[kcols]
    w_v = np.ascontiguousarray(w_kv[:, 256:512])
    b_v = b_kv[256:512]

    shared = {
        "ada_w": ada_w, "kv_ada_w": f(inputs["kv_ada_w"]), "w_in": f(inputs["conv_w_in"])[0],
        "w_out": f(inputs["conv_w_out"])[0], "w_up": f(inputs["mlp_up"]), "w_down": f(inputs["mlp_down"]),
        "w_kdup": w_kdup, "w_v": w_v, "w_q": w_q_p, "w_o": f(inputs["w_o"])[0],
    }
    ident = np.eye(128, dtype=np.float32)
    in_maps = []
    for r in range(n_cores):
        b, hf = r // 2, r % 2
        p0 = hf * TOK
        xt = np.zeros((D, LTOK), np.float32)
        lo = p0 - HALO
        if lo >= 0:
            xt[:, :] = x[b, lo:p0 + TOK, :].T
        else:
            xt[:, HALO:] = x[b, 0:TOK, :].T
        vec = np.zeros((128, NV), np.float32)

        def put(name, arr):
            o, w = VOFF[name]
            assert arr.shape == (128, w), (name, arr.shape, w)
            vec[:, o:o + w] = arr
        put("c", _col(c[b]))
        for l in range(2):
            for i in range(2):
                put(f"ada_b{l}{i}", _col(ada_b[l, i]))
                put(f"gpre{l}{i}", _col(norm_pre[l, i]))
                put(f"gpost{l}{i}", _col(norm_post[l, i]))
        put("kv_ada_b", _col(inputs["kv_ada_b"]))
        put("kv_norm", _col(inputs["kv_norm"]))
        put("cw0", _col(conv_w[0]))
        put("cw1", _col(conv_w[1]))
        put("cw2", _col(conv_w[2]))
        put("bk", _col(b_kdup))
        put("bq", _col(b_q_p))
        put("bo", _col(b_o))
        put("sinks", np.broadcast_to(sinks[None, :], (128, 32)))
        put("flag", np.full((128, 1), float(hf), np.float32))
        put("bv", np.broadcast_to(b_v[None, :], (128, 256)))
        C, Sg = _rope_tables(p0)
        m2, m0 = _masks(hf == 0)
        cst = np.concatenate([ident, m2, m0], axis=1).astype(np.float32)
        m = {"xT": xt, "vecs": vec, "ropeC": C, "ropeS": Sg, "cst": cst}
        m.update(shared)
        in_maps.append(m)
    return in_maps


def kernel(**inputs):
    n = 8
    in_maps = _prep(inputs, n)
    nc = build_nc()
    res = run_bass_kernel_spmd(nc, in_maps, core_ids=list(range(n)))
    out = np.empty((NB, SEQ, D), np.float32)
    for r in range(n):
        b, hf = r // 2, r % 2
        out[b, hf * TOK:(hf + 1) * TOK, :] = res.results[r]["outT"].T
    return out
```

```python
import contextlib
import numpy as np
import concourse.bass as bass
import concourse.mybir as mybir
from concourse.bass_utils import run_bass_kernel_spmd

F32 = mybir.dt.float32
BF16 = mybir.dt.bfloat16
ALU = mybir.AluOpType
AF = mybir.ActivationFunctionType
AX = mybir.AxisListType

D = 2048
KC = 16
DFF = 8192
SEQ = 4096
NB = 4
NH = 32
HD = 64
HALO = 130
TM = 512
TOK = 2048
LTOK = TOK + HALO
EPS = 1e-6
NEG = -30000.0
NSLOT = 4

PERM = np.array(list(range(0, 8)) + list(range(16, 40)) + list(range(8, 16)) + list(range(40, 64)))


def _vec_layout():
    off = {}
    cur = 0

    def add(name, w):
        nonlocal cur
        off[name] = (cur, w)
        cur += w

    add("c", 16)
    for l in range(2):
        for i in range(2):
            add(f"ada_b{l}{i}", 48)
            add(f"gpre{l}{i}", 16)
            add(f"gpost{l}{i}", 16)
    add("kv_ada_b", 32)
    add("kv_norm", 16)
    add("cw0", 16)
    add("cw1", 16)
    add("cw2", 16)
    add("bk", 4)
    add("bq", 16)
    add("bo", 16)
    add("sinks", 32)
    add("flag", 1)
    add("bv", 256)
    return off, cur


VOFF, NV = _vec_layout()


def _col(v):
    return np.ascontiguousarray(np.asarray(v, np.float32).reshape(-1, 128).T)


class Res:
    __slots__ = ("w", "r")

    def __init__(self, init=None):
        self.w = dict(init) if init else {}
        self.r = {}


def _merge(d, t):
    for k, v in t.items():
        if d.get(k, 0) < v:
            d[k] = v


class Eng:
    def __init__(self, nc, es, h, name):
        self.h = h
        self.name = name
        self.sem = es.enter_context(nc.semaphore("e_" + name))
        self.cnt = 0
        self.seen = {}


class Sched:
    def __init__(self, nc, es):
        self.nc = nc
        self.pe = Eng(nc, es, nc.tensor, "pe")
        self.act = Eng(nc, es, nc.scalar, "act")
        self.dve = Eng(nc, es, nc.vector, "dve")
        self.pool = Eng(nc, es, nc.gpsimd, "pool")
        self.sp = Eng(nc, es, nc.sync, "sp")

    def _need(self, reads, writes):
        need = {}
        for r in reads:
            _merge(need, r.w)
        for w in writes:
            _merge(need, w.w)
            _merge(need, w.r)
        return need

    def _waits(self, eng, need):
        for sem, v in need.items():
            if eng is self.pe and sem is eng.sem:
                continue
            if eng.seen.get(sem, 0) >= v:
                continue
            eng.h.wait_ge(sem, v)
            eng.seen[sem] = v

    def _commit(self, tok, reads, writes):
        for r in reads:
            _merge(r.r, tok)
        for w in writes:
            _merge(w.w, tok)
            w.r = {}

    def op(self, eng, fn, reads=(), writes=()):
        self._waits(eng, self._need(reads, writes))
        ins = fn()
        eng.cnt += 1
        ins.then_inc(eng.sem, 1)
        tok = {eng.sem: eng.cnt}
        self._commit(tok, reads, writes)
        return tok

    def group(self, eng, fns, reads=(), writes=()):
        self._waits(eng, self._need(reads, writes))
        ins = None
        for f in fns:
            ins = f()
        eng.cnt += 1
        ins.then_inc(eng.sem, 1)
        tok = {eng.sem: eng.cnt}
        self._commit(tok, reads, writes)
        return tok

    def dma(self, eng, out, in_, sem, semcnt, reads=(), writes=()):
        self._waits(eng, self._need(reads, writes))
        ins = eng.h.dma_start(out=out, in_=in_)
        semcnt[0] += 16
        ins.then_inc(sem, 16)
        tok = {sem: semcnt[0]}
        self._commit(tok, reads, writes)
        return tok


def build_nc(n_tiles=4, do_l1=True, stop=None):
    nc = bass.Bass("TRN2", target_bir_lowering=False)

    def din(name, shape):
        return nc.dram_tensor(name, list(shape), F32, kind="ExternalInput").ap()

    xT = din("xT", [D, LTOK])
    vecs = din("vecs", [128, NV])
    ropeC = din("ropeC", [128, LTOK])
    ropeS = din("ropeS", [128, LTOK])
    cst = din("cst", [128, 128 + 512 + 512])
    ada_w = din("ada_w", [2, 2, D, 3 * D])
    kv_ada_w = din("kv_ada_w", [D, 2 * D])
    w_in = din("w_in", [D, 3 * D])
    w_out = din("w_out", [D, D])
    w_up = din("w_up", [2, D, DFF])
    w_down = din("w_down", [2, DFF, D])
    w_kdup = din("w_kdup", [D, 512])
    w_v = din("w_v", [D, 256])
    w_q = din("w_q", [D, D])
    w_o = din("w_o", [D, D])
    outT = nc.dram_tensor("outT", [D, max(1, n_tiles) * TM], F32, kind="ExternalOutput").ap()

    wsrc = {"w_in": w_in, "w_out": w_out, "up0": w_up[0], "down0": w_down[0], "wk": w_kdup, "wv": w_v,
            "wq": w_q, "wo": w_o, "up1": w_up[1], "down1": w_down[1]}
    worder = ["w_in", "w_out", "up0", "down0", "wk", "wv", "wq", "wo", "up1", "down1"]
    wscr = {k: nc.dram_tensor("scr_" + k, list(v.shape), BF16, kind="Internal").ap() for k, v in wsrc.items()}

    es = contextlib.ExitStack()
    with es:
        sc = Sched(nc, es)
        PE, ACT, DVE, POOL, SP = sc.pe, sc.act, sc.dve, sc.pool, sc.sp

        def sb(name, shape, dt):
            return es.enter_context(nc.sbuf_tensor(name, list(shape), dt))

        X = sb("X", [128, KC * TM], F32)
        XA = sb("XA", [128, 2 * KC * TM], BF16)
        BIG = sb("BIG", [128, 64 * TM + 128], BF16)
        WR = [sb(f"wr{i}", [128, 16, 256], BF16) for i in range(NSLOT)]
        KT = sb("KT", [128, 4, 2, 640], BF16)
        V = sb("V", [128, 5, 256], BF16)
        CT = sb("CT", [128, TM], F32)
        ST = sb("ST", [128, TM], F32)
        VEC = sb("VEC", [128, NV], F32)
        MOD = sb("MOD", [128, 4 * 48 + 32], F32)
        DER = sb("DER", [128, 16 * 16], F32)
        IDN = sb("IDN", [128, 128], BF16)
        MSK = sb("MSK", [128, 1024], BF16)
        ONES = sb("ONES", [128, 128], BF16)
        CACT = sb("CACT", [128, 16], BF16)
        RSTD = sb("RSTD", [128, TM], F32)
        TMP = [sb(f"tmp{i}", [128, TM], F32) for i in range(2)]
        TB = [sb(f"tb{i}", [128, TM], F32) for i in range(2)]
        SQ = [sb(f"sq{i}", [128, TM], BF16) for i in range(3)]
        ZT = sb("ZT", [128, 16, 2], F32)
        PF = [sb(f"pf{i}", [128, 512], F32) for i in range(2)]
        PN = [sb(f"pn{i}", [128, 512], BF16) for i in range(2)]
        PTS = [sb(f"pts{i}", [128, 512], BF16) for i in range(2)]
        SM = [sb(f"sm{i}", [128, 16], F32) for i in range(2)]
        SINK8 = sb("SINK8", [128, 32], F32)

        PS = [es.enter_context(nc.psum_tensor(f"ps{i}", [128, 512], F32)) for i in range(8)]

        rX = [Res() for _ in range(KC)]
        rXA = [Res() for _ in range(32)]
        rBIG = [Res() for _ in range(64)]
        rWR = [Res() for _ in range(NSLOT)]
        rPS = [Res() for _ in range(8)]
        rPTP = [Res(), Res()]
        rKT = Res()
        rV = [Res() for _ in range(5)]
        rCT = Res()
        rVEC = Res()
        rMOD = Res()
        rDER = Res()
        rCST = Res()
        rCACT = Res()
        rRSTD = Res()
        rTMP = [Res() for _ in range(3)]
        rTB = [Res() for _ in range(2)]
        rSQ = [Res() for _ in range(3)]
        rZT = Res()
        rPF = [Res(), Res()]
        rPN = [Res(), Res()]
        rPTS = [Res(), Res()]
        rSM = [Res(), Res()]
        rSINK = Res()

        wsem = [es.enter_context(nc.semaphore(f"wsem{i}")) for i in range(NSLOT)]
        wcnt = [[0] for _ in range(NSLOT)]
        whsem = [es.enter_context(nc.semaphore(f"whsem{i}")) for i in range(NSLOT)]
        whcnt = [[0] for _ in range(NSLOT)]
        pcsem = {k: es.enter_context(nc.semaphore("pc_" + k)) for k in worder}
        pccnt = {k: [0] for k in worder}
        rSCR = {k: Res() for k in worder}
        pcstate = {"n": 0, "look": 2}

        def precast_upto(idx):
            while pcstate["n"] <= min(idx, len(worder) - 1):
                k = worder[pcstate["n"]]
                pcstate["n"] += 1
                src, dst = wsrc[k], wscr[k]
                rows = src.shape[0]
                step = 256
                for r0 in range(0, rows, step):
                    sc.dma(POOL, dst[r0:r0 + step, :], src[r0:r0 + step, :], pcsem[k], pccnt[k], writes=[rSCR[k]])

        csem = es.enter_context(nc.semaphore("csem"))
        ccnt = [0]
        csem2 = es.enter_context(nc.semaphore("csem2"))
        ccnt2 = [0]
        xsemg = [es.enter_context(nc.semaphore(f"xsem{i}")) for i in range(4)]
        xcntg = [[0] for _ in range(4)]
        tsem = es.enter_context(nc.semaphore("tsem"))
        tcnt = [0]
        osem = es.enter_context(nc.semaphore("osem"))
        ocnt = [0]

        def vcol(name, j=0, w=1):
            o, _ = VOFF[name]
            return VEC[:, o + j:o + j + w]

        def x_ch(c, T):
            return X[:, c * T:(c + 1) * T]

        def xa_ch(which, c, T):
            base = which * KC * TM
            return XA[:, base + c * T: base + (c + 1) * T]

        def rxa(which, c):
            return rXA[which * 16 + c]

        def y_xa(c, T):
            v = XA[:, :].bitcast(F32)
            return v[:, c * T:(c + 1) * T]

        def ry_xa(c):
            return [rXA[2 * c], rXA[2 * c + 1]]

        def big_bf(g, T):
            return BIG[:, g * T:(g + 1) * T]

        def big_f32(j, T, base_g=0, extra=0):
            v = BIG[:, :].bitcast(F32)
            o = base_g * TM // 2
            return v[:, o + j * (T + extra): o + (j + 1) * (T + extra)]

        def rbig_f32(j, base_g=0):
            return [rBIG[base_g + 2 * j], rBIG[base_g + 2 * j + 1]]

        sc.dma(SP, VEC[:, :], vecs, csem, ccnt, writes=[rVEC])
        sc.dma(POOL, IDN[:, :], cst[:, 0:128], csem2, ccnt2, writes=[rCST])
        sc.dma(POOL, MSK[:, :], cst[:, 128:1152], csem2, ccnt2, writes=[rCST])
        sc.op(DVE, lambda: nc.vector.memset(ONES[:, :], 1.0), writes=[rCST])
        sc.op(DVE, lambda: nc.vector.memset(ZT[:, :, :], 0.0), writes=[rZT])
        sc.op(DVE, lambda: nc.vector.memset(KT[:, :, :, :], 0.0), writes=[rKT])
        sc.op(ACT, lambda: nc.scalar.activation(CACT[:, :], vcol("c", 0, 16), AF.Silu), reads=[rVEC], writes=[rCACT])
        sc.op(DVE, lambda: nc.vector.tensor_scalar(SINK8[:, :], vcol("sinks", 0, 32), 8.0, None, ALU.mult),
              reads=[rVEC], writes=[rSINK])

        wstate = {"n": 0}

        def load_slab(src, wkey=None):
            s = wstate["n"] % NSLOT
            wstate["n"] += 1
            kc, ncol = src.shape[1], src.shape[2]
            if wkey is None:
                sc.dma(POOL, WR[s][:, 0:kc, 0:ncol], src, wsem[s], wcnt[s], writes=[rWR[s]])
            else:
                sc.dma(SP, WR[s][:, 0:kc, 0:ncol], src, whsem[s], whcnt[s], reads=[rSCR[wkey]], writes=[rWR[s]])
            return WR[s], rWR[s]

        pstate = {"n": 0}

        def next_pset():
            s = pstate["n"] % 3
            pstate["n"] += 1
            return (2 * s, 2 * s + 1)

        def proj_fm(W, krows, ncols, in_aps, in_res, T, epi):
            wkey = None
            if isinstance(W, str):
                wkey = W
                precast_upto(worder.index(wkey) + pcstate["look"])
                W = wscr[wkey]
            KS = krows // 2048
            deferred = []
            for cg in range(ncols // 256):
                pset = next_pset()
                for ks in range(KS):
                    src = W[ks * 2048:(ks + 1) * 2048, cg * 256:(cg + 1) * 256].rearrange("(k p) c -> p k c", p=128)
                    slab, sres = load_slab(src, wkey)
                    fns = []
                    for j in range(2):
                        for ic in range(16):
                            kidx = ks * 16 + ic
                            fns.append(lambda j=j, ic=ic, kidx=kidx, slab=slab, pset=pset, ks=ks:
                                       nc.tensor.matmul(PS[pset[j]][:, 0:T], slab[:, ic, j * 128:(j + 1) * 128],
                                                        in_aps[kidx], start=(ks == 0 and ic == 0),
                                                        stop=(ks == KS - 1 and ic == 15)))
                    sc.group(PE, fns, reads=[sres] + in_res[ks * 16:(ks + 1) * 16],
                             writes=[rPS[pset[0]], rPS[pset[1]]])
                for d in deferred:
                    d()
                deferred = []
                for j in range(2):
                    r = epi(cg * 2 + j, PS[pset[j]][:, 0:T], rPS[pset[j]])
                    if r:
                        deferred.extend(r)
            for d in deferred:
                d()

        sqi = {"n": 0}

        def stats_mm(sq_ap, sq_res, T, first, last):
            sc.group(PE, [lambda: nc.tensor.matmul(PS[6][:, 0:T], ONES[:, :], sq_ap, start=first, stop=last)],
                     reads=[sq_res, rCST], writes=[rPS[6]])

        def finish_rstd(T):
            sc.op(ACT, lambda: nc.scalar.activation(RSTD[:, 0:T], PS[6][:, 0:T], AF.Sqrt, bias=EPS, scale=1.0 / D),
                  reads=[rPS[6]], writes=[rRSTD])
            sc.op(DVE, lambda: nc.vector.reciprocal(RSTD[:, 0:T], RSTD[:, 0:T]), reads=[rRSTD], writes=[rRSTD])

        def prenorm_stats(T):
            for c in range(KC):
                i = sqi["n"] % 3
                sqi["n"] += 1
                if c % 2 == 0:
                    sc.op(ACT, lambda c=c, i=i: nc.scalar.activation(SQ[i][:, 0:T], x_ch(c, T), AF.Square),
                          reads=[rX[c]], writes=[rSQ[i]])
                else:
                    sc.op(DVE, lambda c=c, i=i: nc.vector.tensor_tensor(SQ[i][:, 0:T], x_ch(c, T), x_ch(c, T), ALU.mult),
                          reads=[rX[c]], writes=[rSQ[i]])
                stats_mm(SQ[i][:, 0:T], rSQ[i], T, c == 0, c == KC - 1)
            finish_rstd(T)

        tmi = {"n": 0}

        def modulate(which, T, acol, bcol):
            for c in range(KC):
                i = tmi["n"] % 2
                tmi["n"] += 1
                sc.op(DVE, lambda c=c, i=i: nc.vector.scalar_tensor_tensor(TMP[i][:, 0:T], x_ch(c, T), acol(c), RSTD[:, 0:T],
                                                                          ALU.mult, ALU.mult),
                      reads=[rX[c], rRSTD, rDER, rMOD], writes=[rTMP[i]])
                sc.op(ACT, lambda c=c, i=i: nc.scalar.activation(xa_ch(which, c, T), TMP[i][:, 0:T], AF.Identity,
                                                                bias=bcol(c), scale=1.0),
                      reads=[rTMP[i], rMOD, rDER], writes=[rxa(which, c)])

        def post_epi_factory(ydst, rydst, T, biascol=None):
            def epi(oc, ps, pres):
                if biascol is None:
                    sc.op(ACT, lambda: nc.scalar.copy(ydst(oc), ps), reads=[pres], writes=rydst(oc))
                else:
                    sc.op(ACT, lambda: nc.scalar.activation(ydst(oc), ps, AF.Identity, bias=biascol(oc), scale=1.0),
                          reads=[pres, rVEC], writes=rydst(oc))
                i = sqi["n"] % 3
                sqi["n"] += 1
                sc.op(DVE, lambda: nc.vector.tensor_tensor(SQ[i][:, 0:T], ydst(oc), ydst(oc), ALU.mult),
                      reads=rydst(oc), writes=[rSQ[i]])
                return [lambda: stats_mm(SQ[i][:, 0:T], rSQ[i], T, oc == 0, oc == KC - 1)]
            return epi

        def postnorm_apply(ysrc, rysrc, T, gcol, dst=None, rdst=None):
            finish_rstd(T)
            for c in range(KC):
                i = tmi["n"] % 2
                tmi["n"] += 1
                sc.op(DVE, lambda c=c, i=i: nc.vector.scalar_tensor_tensor(TMP[i][:, 0:T], ysrc(c), gcol(c), RSTD[:, 0:T],
                                                                          ALU.mult, ALU.mult),
                      reads=rysrc(c) + [rRSTD, rDER], writes=[rTMP[i]])
                if dst is None:
                    sc.op(DVE, lambda c=c, i=i: nc.vector.tensor_tensor(x_ch(c, T), x_ch(c, T), TMP[i][:, 0:T], ALU.add),
                          reads=[rTMP[i]], writes=[rX[c]])
                else:
                    sc.op(DVE, lambda c=c, i=i: nc.vector.tensor_tensor(dst(c), x_ch(c, T), TMP[i][:, 0:T], ALU.add),
                          reads=[rTMP[i], rX[c]], writes=rdst(c))

        cact_aps = [CACT[:, c:c + 1] for c in range(KC)]
        cact_res = [rCACT] * KC

        def ada_stage(W, ncols, modoff, bname):
            def epi(oc, ps, pres):
                sc.op(DVE, lambda: nc.vector.tensor_tensor(MOD[:, modoff + oc:modoff + oc + 1], ps, vcol(bname, oc, 1), ALU.add),
                      reads=[pres, rVEC], writes=[rMOD])
                return None
            proj_fm(W, 2048, ncols, cact_aps, cact_res, 1, epi)

        for l in range(2):
            for i in range(2):
                ada_stage(ada_w[l, i], 3 * D, (l * 2 + i) * 48, f"ada_b{l}{i}")
                if l == 0 and i == 0:
                    precast_upto(3)
        ada_stage(kv_ada_w, 2 * D, 192, "kv_ada_b")

        def der(j, w=16):
            return DER[:, j * 16:j * 16 + w]

        for l in range(2):
            for i in range(2):
                k = l * 2 + i
                mo = k * 48
                sc.op(DVE, lambda: nc.vector.tensor_scalar(der(2 * k), MOD[:, mo + 16:mo + 32], 1.0, None, ALU.add),
                      reads=[rMOD], writes=[rDER])
                sc.op(DVE, lambda: nc.vector.tensor_tensor(der(2 * k), der(2 * k), vcol(f"gpre{l}{i}", 0, 16), ALU.mult),
                      reads=[rDER, rVEC], writes=[rDER])
                sc.op(DVE, lambda: nc.vector.tensor_tensor(der(2 * k + 1), MOD[:, mo + 32:mo + 48], vcol(f"gpost{l}{i}", 0, 16), ALU.mult),
                      reads=[rMOD, rVEC], writes=[rDER])
        sc.op(DVE, lambda: nc.vector.tensor_scalar(der(8), MOD[:, 192 + 16:192 + 32], 1.0, None, ALU.add),
              reads=[rMOD], writes=[rDER])
        sc.op(DVE, lambda: nc.vector.tensor_tensor(der(8), der(8), vcol("kv_norm", 0, 16), ALU.mult),
              reads=[rDER, rVEC], writes=[rDER])

        if stop == "ada":
            sc.dma(POOL, outT[0:128, 0:224], MOD[:, :], osem, ocnt, reads=[rMOD, rDER])
            nc.gpsimd.wait_ge(osem, ocnt[0])
            return nc

        def Acol(l, i):
            return lambda c: DER[:, (2 * (l * 2 + i)) * 16 + c:(2 * (l * 2 + i)) * 16 + c + 1]

        def Gcol(l, i):
            return lambda c: DER[:, (2 * (l * 2 + i) + 1) * 16 + c:(2 * (l * 2 + i) + 1) * 16 + c + 1]

        def Bcol(l, i):
            return lambda c: MOD[:, (l * 2 + i) * 48 + c:(l * 2 + i) * 48 + c + 1]

        def Akv(c):
            return DER[:, 128 + c:128 + c + 1]

        def Bkv(c):
            return MOD[:, 192 + c:192 + c + 1]

        def mlp(l, T, final=False):
            prenorm_stats(T)
            modulate(0, T, Acol(l, 1), Bcol(l, 1))

            def up_epi(oc, ps, pres):
                i = tmi["n"] % 2
                tmi["n"] += 1
                sc.op(ACT, lambda: nc.scalar.activation(TMP[i][:, 0:T], ps, AF.Relu), reads=[pres], writes=[rTMP[i]])
                sc.op(DVE, lambda: nc.vector.tensor_tensor(big_bf(oc, T), TMP[i][:, 0:T], TMP[i][:, 0:T], ALU.mult),
                      reads=[rTMP[i]], writes=[rBIG[oc]])
                return None
            proj_fm(f"up{l}", 2048, DFF, [xa_ch(0, c, T) for c in range(KC)], [rxa(0, c) for c in range(KC)], T, up_epi)
            proj_fm(f"down{l}", DFF, D, [big_bf(g, T) for g in range(64)], [rBIG[g] for g in range(64)], T,
                    post_epi_factory(lambda oc: y_xa(oc, T), ry_xa, T))
            if final:
                postnorm_apply(lambda c: y_xa(c, T), ry_xa, T, Gcol(l, 1), dst=lambda c: big_f32(c, T), rdst=rbig_f32)
            else:
                postnorm_apply(lambda c: y_xa(c, T), ry_xa, T, Gcol(l, 1))

        def rope_chunk(src, rsrc, dst, rdst, T, c0, n):
            i = tmi["n"] % 2
            tmi["n"] += 1
            t2 = TMP[i]
            for q in range(4):
                qs = q ^ 1
                sc.op(DVE, lambda q=q, qs=qs: nc.vector.tensor_tensor(t2[q * 32:(q + 1) * 32, 0:n], src[qs * 32:(qs + 1) * 32, c0:c0 + n],
                                                                     ST[qs * 32:(qs + 1) * 32, c0:c0 + n], ALU.mult),
                      reads=[rsrc, rCT], writes=[rTMP[i]])
            sc.op(DVE, lambda: nc.vector.tensor_tensor(src[:, c0:c0 + n], src[:, c0:c0 + n], CT[:, c0:c0 + n], ALU.mult),
                  reads=[rsrc, rCT], writes=[rsrc])
            dl = dst if isinstance(dst, list) else [((0, 128), dst)]
            for (r0, r1), dap in dl:
                sc.op(DVE, lambda r0=r0, r1=r1, dap=dap: nc.vector.tensor_tensor(dap, src[r0:r1, c0:c0 + n], t2[r0:r1, 0:n], ALU.add),
                      reads=[rsrc, rTMP[i]], writes=rdst)

        tiles = [("halo", 0, HALO)] + [("main", HALO + j * TM, TM) for j in range(n_tiles)]
        tbi = {"n": 0}
        for tix, (kind, t0, T) in enumerate(tiles):
            is_halo = kind == "halo"
            pcstate["look"] = 0 if is_halo else 2
            for gq in range(4):
                sc.dma(POOL, X[:, gq * 4 * T:(gq + 1) * 4 * T].rearrange("p (c t) -> p c t", t=T),
                       xT[gq * 512:(gq + 1) * 512, t0:t0 + T].rearrange("(c p) t -> p c t", p=128), xsemg[gq], xcntg[gq],
                       writes=(rX if tix <= 1 else rX[gq * 4:(gq + 1) * 4]))
            sc.dma(POOL, CT[:, 0:T], ropeC[:, t0:t0 + T], tsem, tcnt, writes=[rCT])
            sc.dma(POOL, ST[:, 0:T], ropeS[:, t0:t0 + T], tsem, tcnt, writes=[rCT])

            prenorm_stats(T)
            modulate(0, T, Acol(0, 0), Bcol(0, 0))
            ZW = T + 2

            def zch(i):
                return big_f32(i, T, base_g=32, extra=2)

            def rz(i):
                return [rBIG[32 + 2 * i], rBIG[32 + 2 * i + 1], rBIG[min(63, 32 + 2 * i + 2)]]

            zall = BIG[:, :].bitcast(F32)[:, 16 * TM:16 * TM + 16 * ZW].rearrange("p (c t) -> p c t", t=ZW)
            rzall = [rBIG[g] for g in range(32, 64)]
            sc.op(DVE, lambda: nc.vector.tensor_copy(zall[:, :, 0:2], ZT[:, :, :]), reads=[rZT], writes=rzall)

            def conv_chunk(i):
                k = tmi["n"] % 2
                tmi["n"] += 1
                z = zch(i)
                sc.op(ACT, lambda: nc.scalar.activation(TMP[k][:, 0:T], z[:, 2:ZW], AF.Copy, scale=vcol("cw2", i, 1)),
                      reads=rz(i) + [rVEC], writes=[rTMP[k]])
                sc.op(DVE, lambda: nc.vector.scalar_tensor_tensor(TMP[k][:, 0:T], z[:, 1:ZW - 1], vcol("cw1", i, 1), TMP[k][:, 0:T],
                                                                 ALU.mult, ALU.add),
                      reads=rz(i) + [rVEC, rTMP[k]], writes=[rTMP[k]])
                sc.op(DVE, lambda: nc.vector.scalar_tensor_tensor(TMP[k][:, 0:T], z[:, 0:ZW - 2], vcol("cw0", i, 1), TMP[k][:, 0:T],
                                                                 ALU.mult, ALU.add),
                      reads=rz(i) + [rVEC, rTMP[k]], writes=[rTMP[k]])
                sc.op(DVE, lambda: nc.vector.tensor_tensor(xa_ch(1, i, T), TMP[k][:, 0:T], big_f32(i, T), ALU.mult),
                      reads=[rTMP[k]] + rbig_f32(i), writes=[rxa(1, i)])

            def win_epi(oc, ps, pres):
                if oc < 16:
                    sc.op(ACT, lambda: nc.scalar.copy(big_f32(oc, T), ps), reads=[pres], writes=rbig_f32(oc))
                elif oc < 32:
                    i = oc - 16
                    sc.op(ACT, lambda: nc.scalar.copy(zch(i)[:, 2:ZW], ps), reads=[pres], writes=rz(i))
                else:
                    i = oc - 32
                    sc.op(DVE, lambda: nc.vector.tensor_tensor(zch(i)[:, 2:ZW], zch(i)[:, 2:ZW], ps, ALU.mult),
                          reads=[pres] + rz(i), writes=rz(i))
                    conv_chunk(i)
                return None
            proj_fm("w_in", 2048, 3 * D, [xa_ch(0, c, T) for c in range(KC)], [rxa(0, c) for c in range(KC)], T, win_epi)

            if is_halo:
                sc.op(DVE, lambda: nc.vector.tensor_scalar(ZT[:, :, :], zall[:, :, T:T + 2], vcol("flag", 0, 1), None, ALU.mult),
                      reads=rzall + [rVEC], writes=[rZT])
            else:
                sc.op(DVE, lambda: nc.vector.tensor_copy(ZT[:, :, :], zall[:, :, T:T + 2]), reads=rzall, writes=[rZT])

            proj_fm("w_out", 2048, D, [xa_ch(1, c, T) for c in range(KC)], [rxa(1, c) for c in range(KC)], T,
                    post_epi_factory(lambda oc: big_f32(oc, T), rbig_f32, T))
            postnorm_apply(lambda c: big_f32(c, T), rbig_f32, T, Gcol(0, 0))

            mlp(0, T)

            if not do_l1:
                if is_halo and n_tiles == 0:
                    sc.dma(POOL, outT[:, 0:T].rearrange("(c p) t -> p c t", p=128),
                           X[:, 0:KC * T].rearrange("p (c t) -> p c t", t=T), osem, ocnt, reads=rX)
                if not is_halo:
                    j = tix - 1
                    sc.dma(POOL, outT[:, j * TM:(j + 1) * TM].rearrange("(c p) t -> p c t", p=128),
                           X[:, 0:KC * T].rearrange("p (c t) -> p c t", t=T), osem, ocnt, reads=rX)
                continue

            prenorm_stats(T)
            modulate(0, T, Akv, Bkv)
            if not is_halo:
                modulate(1, T, Acol(1, 0), Bcol(1, 0))

            def k_epi(g, ps, pres):
                i = tbi["n"] % 2
                tbi["n"] += 1
                sc.op(ACT, lambda: nc.scalar.activation(TB[i][:, 0:T], ps, AF.Identity, bias=vcol("bk", g, 1), scale=1.0),
                      reads=[pres, rVEC], writes=[rTB[i]])
                if is_halo:
                    rope_chunk(TB[i], rTB[i], [((0, 64), KT[0:64, g, 0, 0:128]), ((64, 128), KT[64:128, g, 1, 0:128])], [rKT], T, 2, 128)
                else:
                    rope_chunk(TB[i], rTB[i], [((0, 64), KT[0:64, g, 0, 128:640]), ((64, 128), KT[64:128, g, 1, 128:640])], [rKT], T, 0, T)
                return None
            proj_fm("wk", 2048, 512, [xa_ch(0, c, T) for c in range(KC)], [rxa(0, c) for c in range(KC)], T, k_epi)

            precast_upto(worder.index("wv") + pcstate["look"])
            vslab, vres = load_slab(wscr["wv"].rearrange("(k p) c -> p k c", p=128), "wv")
            nblk = 1 if is_halo else 4
            for tb in range(nblk):
                c0 = 2 if is_halo else tb * 128
                slot = 0 if is_halo else 1 + tb
                pb = 4 + (tb % 2)
                fns = [lambda ic=ic, c0=c0, pb=pb: nc.tensor.matmul(PS[pb][:, 0:256], xa_ch(0, ic, T)[:, c0:c0 + 128], vslab[:, ic, :],
                                                                   start=(ic == 0), stop=(ic == 15)) for ic in range(KC)]
                sc.group(PE, fns, reads=[vres] + [rxa(0, c) for c in range(KC)], writes=[rPS[pb]])
                sc.op(DVE, lambda slot=slot, pb=pb: nc.vector.tensor_tensor(V[:, slot, :], PS[pb][:, 0:256], vcol("bv", 0, 256), ALU.add),
                      reads=[rPS[pb], rVEC], writes=[rV[slot]])
            if is_halo:
                if n_tiles == 0:
                    sc.dma(POOL, outT[:, 0:T].rearrange("(c p) t -> p c t", p=128),
                           X[:, 0:KC * T].rearrange("p (c t) -> p c t", t=T), osem, ocnt, reads=rX)
                continue

            def q_epi(oc, ps, pres):
                i = tbi["n"] % 2
                tbi["n"] += 1
                sc.op(ACT, lambda: nc.scalar.activation(TB[i][:, 0:T], ps, AF.Identity, bias=vcol("bq", oc, 1), scale=1.0),
                      reads=[pres, rVEC], writes=[rTB[i]])
                rope_chunk(TB[i], rTB[i], big_bf(oc, T), [rBIG[oc]], T, 0, T)
                return None
            proj_fm("wq", 2048, D, [xa_ch(1, c, T) for c in range(KC)], [rxa(1, c) for c in range(KC)], T, q_epi)

            if stop == "q":
                sc.dma(POOL, outT[:, 0:T].rearrange("(c p) t -> p c t", p=128),
                       X[:, 0:KC * T].rearrange("p (c t) -> p c t", t=T), osem, ocnt, reads=rX + [rBIG[g_] for g_ in range(16)])
                nc.gpsimd.wait_ge(osem, ocnt[0])
                return nc
            def emit_S1(c, qb, a):
                g = c // 4
                sb_ = a
                first_blk = (tix == 1 and qb == 0)
                moff = 512 if first_blk else 0
                q0 = qb * 128
                fns = [
                    lambda: nc.tensor.matmul(PS[sb_][:, 0:512], IDN[:, :], MSK[:, moff:moff + 512], start=True, stop=False),
                    lambda: nc.tensor.matmul(PS[sb_][:, 0:256], big_bf(c, T)[:, q0:q0 + 128],
                                             KT[:, g, 0, q0:q0 + 256], start=False, stop=False),
                    lambda: nc.tensor.matmul(PS[sb_][:, 256:512], big_bf(c, T)[:, q0:q0 + 128],
                                             KT[:, g, 1, q0:q0 + 256], start=False, stop=True),
                ]
                sc.group(PE, fns, reads=[rCST, rBIG[c], rKT], writes=[rPS[sb_]])
                sm = SM[a]
                sc.op(DVE, lambda: nc.vector.reduce_max(sm[:, 0:2], PS[sb_][:, 0:512].rearrange("p (h k) -> p h k", h=2), AX.X),
                      reads=[rPS[sb_]], writes=[rSM[a]])
                sc.op(DVE, lambda: nc.vector.tensor_tensor(sm[:, 0:2], sm[:, 0:2], SINK8[:, 2 * c:2 * c + 2], ALU.max),
                      reads=[rSM[a], rSINK], writes=[rSM[a]])
                sc.op(DVE, lambda: nc.vector.tensor_scalar(sm[:, 2:4], sm[:, 0:2], -0.125, None, ALU.mult),
                      reads=[rSM[a]], writes=[rSM[a]])
                sc.op(DVE, lambda: nc.vector.memset(sm[:, 4:6], 0.0), writes=[rSM[a]])
                for hh in range(2):
                    sc.op(ACT, lambda hh=hh: nc.scalar.activation(PF[a][:, hh * 256:(hh + 1) * 256], PS[sb_][:, hh * 256:(hh + 1) * 256],
                                                                  AF.Exp, bias=sm[:, 2 + hh:3 + hh], scale=0.125,
                                                                  accum_out=sm[:, 4 + hh:5 + hh]),
                          reads=[rPS[sb_], rSM[a]], writes=[rPF[a], rSM[a]])
                    sc.op(ACT, lambda hh=hh: nc.scalar.activation(sm[:, 6 + hh:7 + hh], vcol("sinks", 2 * c + hh, 1), AF.Exp,
                                                                  bias=sm[:, 2 + hh:3 + hh], scale=1.0),
                          reads=[rSM[a], rVEC], writes=[rSM[a]])

            def emit_S2(c, qb, a):
                sm = SM[a]
                sc.op(DVE, lambda: nc.vector.tensor_tensor(sm[:, 8:10], sm[:, 4:6], sm[:, 6:8], ALU.add),
                      reads=[rSM[a]], writes=[rSM[a]])
                sc.op(DVE, lambda: nc.vector.reciprocal(sm[:, 8:10], sm[:, 8:10]), reads=[rSM[a]], writes=[rSM[a]])
                for hh in range(2):
                    sc.op(DVE, lambda hh=hh: nc.vector.tensor_scalar(PN[a][:, hh * 256:(hh + 1) * 256], PF[a][:, hh * 256:(hh + 1) * 256],
                                                                     sm[:, 8 + hh:9 + hh], None, ALU.mult),
                          reads=[rPF[a], rSM[a]], writes=[rPN[a]])

            def emit_T(c, qb, a):
                fns = [lambda k=k: nc.tensor.matmul(PS[4 + a][:, k * 128:(k + 1) * 128], PN[a][:, k * 128:(k + 1) * 128], IDN[:, :],
                                                    start=True, stop=True)
                       for k in range(4)]
                sc.group(PE, fns, reads=[rPN[a], rCST], writes=[rPS[4 + a]])
                sc.op(ACT, lambda: nc.scalar.copy(PTS[a][:, :], PS[4 + a][:, 0:512]), reads=[rPS[4 + a]], writes=[rPTS[a]])

            def emit_PV(c, qb, a):
                g = c // 4
                q0 = qb * 128
                ob = (2, 3) if c % 2 == 0 else (6, 7)
                fns = []
                for hh in range(2):
                    for kb in range(2):
                        vs = qb + kb
                        fns.append(lambda hh=hh, kb=kb, vs=vs:
                                   nc.tensor.matmul(PS[ob[hh]][0:64, q0:q0 + 128], V[:, vs, g * 64:(g + 1) * 64],
                                                    PTS[a][:, (hh * 2 + kb) * 128:(hh * 2 + kb + 1) * 128],
                                                    start=(kb == 0), stop=(kb == 1)))
                sc.group(PE, fns, reads=[rPTS[a], rV[qb], rV[qb + 1]], writes=[rPS[ob[0]], rPS[ob[1]]])
                if qb == 3:
                    sc.op(ACT, lambda: nc.scalar.copy(big_bf(16 + c, T)[0:64, :], PS[ob[0]][0:64, 0:T]), reads=[rPS[ob[0]]], writes=[rBIG[16 + c]])
                    sc.op(ACT, lambda: nc.scalar.copy(big_bf(16 + c, T)[64:128, :], PS[ob[1]][0:64, 0:T]), reads=[rPS[ob[1]]], writes=[rBIG[16 + c]])

            pairs = [(c, qb, (c * 4 + qb) % 2) for c in range(KC) for qb in range(4)]
            NP_ = len(pairs)
            for pi in range(NP_ + 3):
                if pi < NP_:
                    emit_S1(*pairs[pi])
                if 1 <= pi <= NP_:
                    emit_S2(*pairs[pi - 1])
                if 2 <= pi <= NP_ + 1:
                    emit_T(*pairs[pi - 2])
                if pi >= 3:
                    emit_PV(*pairs[pi - 3])

            if stop == "att":
                sc.dma(POOL, outT[:, 0:T].rearrange("(c p) t -> p c t", p=128),
                       X[:, 0:KC * T].rearrange("p (c t) -> p c t", t=T), osem, ocnt, reads=rX + [rBIG[g_] for g_ in range(32)])
                nc.gpsimd.wait_ge(osem, ocnt[0])
                return nc
            sc.op(DVE, lambda: nc.vector.tensor_copy(KT[:, :, :, 0:128], KT[:, :, :, 512:640]), reads=[rKT], writes=[rKT])
            sc.op(DVE, lambda: nc.vector.tensor_copy(V[:, 0, :], V[:, 4, :]), reads=[rV[4]], writes=[rV[0]])

            proj_fm("wo", 2048, D, [big_bf(16 + c, T) for c in range(KC)], [rBIG[16 + c] for c in range(KC)], T,
                    post_epi_factory(lambda oc: big_f32(oc, T, base_g=32), lambda oc: rbig_f32(oc, 32), T,
                                     biascol=lambda oc: vcol("bo", oc, 1)))
            postnorm_apply(lambda c: big_f32(c, T, base_g=32), lambda c: rbig_f32(c, 32), T, Gcol(1, 0))

            mlp(1, T, final=True)

            j = tix - 1
            obig = BIG[:, :].bitcast(F32)[:, 0:KC * T].rearrange("p (c t) -> p c t", t=T)
            sc.dma(POOL, outT[:, j * TM:(j + 1) * TM].rearrange("(c p) t -> p c t", p=128),
                   obig, osem, ocnt, reads=[rBIG[g_] for g_ in range(32)])

        nc.gpsimd.wait_ge(osem, ocnt[0])
    return nc


def _rope_tables(p0):
    inv = (np.float32(500000.0) ** (-np.arange(0, 16, 2, dtype=np.float32) / np.float32(16))).astype(np.float32)
    pos = (np.arange(LTOK, dtype=np.float32) + np.float32(p0 - HALO)).astype(np.float32)
    ang = (pos[:, None] * inv[None, :]).astype(np.float32)
    cos = np.cos(ang).astype(np.float32)
    sin = np.sin(ang).astype(np.float32)
    C = np.ones((128, LTOK), np.float32)
    Sg = np.zeros((128, LTOK), np.float32)
    for r in range(128):
        j = r % 64
        q = j // 32
        jj = j % 32
        if jj < 8:
            C[r] = cos[:, jj]
            Sg[r] = -sin[:, jj] if q == 0 else sin[:, jj]
    Ssw = np.ascontiguousarray(Sg.reshape(2, 2, 32, LTOK)[:, ::-1].reshape(128, LTOK))
    return C, Ssw


def _masks(first_core_half):
    i = np.arange(128)[:, None]
    j = np.arange(128)[None, :]
    mp = np.where(j > i, 0.0, NEG).astype(np.float32)
    mc = np.where(j <= i, 0.0, NEG).astype(np.float32)
    m2 = np.concatenate([mp, mc, mp, mc], axis=1)
    if first_core_half:
        mp0 = np.full((128, 128), NEG, np.float32)
    else:
        mp0 = mp
    m0 = np.concatenate([mp0, mc, mp0, mc], axis=1)
    return m2, m0


def _prep(inputs, n_cores=8):
    f = lambda a: np.ascontiguousarray(np.asarray(a, dtype=np.float32))
    x = f(inputs["x"])
    c = f(inputs["c"])
    ada_w = f(inputs["ada_w"])
    ada_b = f(inputs["ada_b"])
    norm_pre = f(inputs["norm_pre"])
    norm_post = f(inputs["norm_post"])
    conv_w = f(inputs["conv_w"])[0]
    w_kv = f(inputs["w_kv"])
    b_kv = f(inputs["b_kv"])
    w_q = f(inputs["w_q"])[0]
    b_q = f(inputs["b_q"])[0]
    sinks = f(inputs["sinks"])[0]
    b_o = f(inputs["b_o"])[0]

    qcols = np.concatenate([h * 64 + PERM for h in range(NH)])
    w_q_p = np.ascontiguousarray(w_q[:, qcols])
    b_q_p = b_q[qcols]
    kcols = np.concatenate([np.concatenate([g * 64 + PERM, g * 64 + PERM]) for g in range(4)])
    w_kdup = np.ascontiguousarray(w_kv[:, kcols])
    b_kdup = b_kv[kcols]
    w_v = np.ascontiguousarray(w_kv[:, 256:512])
    b_v = b_kv[256:512]

    shared = {
        "ada_w": ada_w, "kv_ada_w": f(inputs["kv_ada_w"]), "w_in": f(inputs["conv_w_in"])[0],
        "w_out": f(inputs["conv_w_out"])[0], "w_up": f(inputs["mlp_up"]), "w_down": f(inputs["mlp_down"]),
        "w_kdup": w_kdup, "w_v": w_v, "w_q": w_q_p, "w_o": f(inputs["w_o"])[0],
    }
    ident = np.eye(128, dtype=np.float32)
    in_maps = []
    for r in range(n_cores):
        b, hf = r // 2, r % 2
        p0 = hf * TOK
        xt = np.zeros((D, LTOK), np.float32)
        lo = p0 - HALO
        if lo >= 0:
            xt[:, :] = x[b, lo:p0 + TOK, :].T
        else:
            xt[:, HALO:] = x[b, 0:TOK, :].T
        vec = np.zeros((128, NV), np.float32)

        def put(name, arr):
            o, w = VOFF[name]
            assert arr.shape == (128, w), (name, arr.shape, w)
            vec[:, o:o + w] = arr
        put("c", _col(c[b]))
        for l in range(2):
            for i in range(2):
                put(f"ada_b{l}{i}", _col(ada_b[l, i]))
                put(f"gpre{l}{i}", _col(norm_pre[l, i]))
                put(f"gpost{l}{i}", _col(norm_post[l, i]))
        put("kv_ada_b", _col(inputs["kv_ada_b"]))
        put("kv_norm", _col(inputs["kv_norm"]))
        put("cw0", _col(conv_w[0]))
        put("cw1", _col(conv_w[1]))
        put("cw2", _col(conv_w[2]))
        put("bk", _col(b_kdup))
        put("bq", _col(b_q_p))
        put("bo", _col(b_o))
        put("sinks", np.broadcast_to(sinks[None, :], (128, 32)))
        put("flag", np.full((128, 1), float(hf), np.float32))
        put("bv", np.broadcast_to(b_v[None, :], (128, 256)))
        C, Sg = _rope_tables(p0)
        m2, m0 = _masks(hf == 0)
        cst = np.concatenate([ident, m2, m0], axis=1).astype(np.float32)
        m = {"xT": xt, "vecs": vec, "ropeC": C, "ropeS": Sg, "cst": cst}
        m.update(shared)
        in_maps.append(m)
    return in_maps


def kernel(**inputs):
    n = 8
    in_maps = _prep(inputs, n)
    nc = build_nc()
    res = run_bass_kernel_spmd(nc, in_maps, core_ids=list(range(n)))
    out = np.empty((NB, SEQ, D), np.float32)
    for r in range(n):
        b, hf = r // 2, r % 2
        out[b, hf * TOK:(hf + 1) * TOK, :] = res.results[r]["outT"].T
    return out
```

```python
import contextlib
import numpy as np
import concourse.bass as bass
import concourse.mybir as mybir
from concourse.bass_utils import run_bass_kernel_spmd

F32 = mybir.dt.float32
BF16 = mybir.dt.bfloat16
ALU = mybir.AluOpType
AF = mybir.ActivationFunctionType
AX = mybir.AxisListType

D = 2048
KC = 16
DFF = 8192
SEQ = 4096
NB = 4
NH = 32
HD = 64
HALO = 130
TM = 512
TOK = 2048
LTOK = TOK + HALO
EPS = 1e-6
NEG = -30000.0
NSLOT = 4

PERM = np.array(list(range(0, 8)) + list(range(16, 40)) + list(range(8, 16)) + list(range(40, 64)))


def _vec_layout():
    off = {}
    cur = 0

    def add(name, w):
        nonlocal cur
        off[name] = (cur, w)
        cur += w

    add("c", 16)
    for l in range(2):
        for i in range(2):
            add(f"ada_b{l}{i}", 48)
            add(f"gpre{l}{i}", 16)
            add(f"gpost{l}{i}", 16)
    add("kv_ada_b", 32)
    add("kv_norm", 16)
    add("cw0", 16)
    add("cw1", 16)
    add("cw2", 16)
    add("bk", 4)
    add("bq", 16)
    add("bo", 16)
    add("sinks", 32)
    add("flag", 1)
    add("bv", 256)
    return off, cur


VOFF, NV = _vec_layout()


def _col(v):
    return np.ascontiguousarray(np.asarray(v, np.float32).reshape(-1, 128).T)


class Res:
    __slots__ = ("w", "r")

    def __init__(self, init=None):
        self.w = dict(init) if init else {}
        self.r = {}


def _merge(d, t):
    for k, v in t.items():
        if d.get(k, 0) < v:
            d[k] = v


class Eng:
    def __init__(self, nc, es, h, name):
        self.h = h
        self.name = name
        self.sem = es.enter_context(nc.semaphore("e_" + name))
        self.cnt = 0
        self.seen = {}


class Sched:
    def __init__(self, nc, es):
        self.nc = nc
        self.pe = Eng(nc, es, nc.tensor, "pe")
        self.act = Eng(nc, es, nc.scalar, "act")
        self.dve = Eng(nc, es, nc.vector, "dve")
        self.pool = Eng(nc, es, nc.gpsimd, "pool")
        self.sp = Eng(nc, es, nc.sync, "sp")

    def _need(self, reads, writes):
        need = {}
        for r in reads:
            _merge(need, r.w)
        for w in writes:
            _merge(need, w.w)
            _merge(need, w.r)
        return need

    def _waits(self, eng, need):
        for sem, v in need.items():
            if eng is self.pe and sem is eng.sem:
                continue
            if eng.seen.get(sem, 0) >= v:
                continue
            eng.h.wait_ge(sem, v)
            eng.seen[sem] = v

    def _commit(self, tok, reads, writes):
        for r in reads:
            _merge(r.r, tok)
        for w in writes:
            _merge(w.w, tok)
            w.r = {}

    def op(self, eng, fn, reads=(), writes=()):
        self._waits(eng, self._need(reads, writes))
        ins = fn()
        eng.cnt += 1
        ins.then_inc(eng.sem, 1)
        tok = {eng.sem: eng.cnt}
        self._commit(tok, reads, writes)
        return tok

    def group(self, eng, fns, reads=(), writes=()):
        self._waits(eng, self._need(reads, writes))
        ins = None
        for f in fns:
            ins = f()
        eng.cnt += 1
        ins.then_inc(eng.sem, 1)
        tok = {eng.sem: eng.cnt}
        self._commit(tok, reads, writes)
        return tok

    def dma(self, eng, out, in_, sem, semcnt, reads=(), writes=()):
        self._waits(eng, self._need(reads, writes))
        ins = eng.h.dma_start(out=out, in_=in_)
        semcnt[0] += 16
        ins.then_inc(sem, 16)
        tok = {sem: semcnt[0]}
        self._commit(tok, reads, writes)
        return tok


def build_nc(n_tiles=4, do_l1=True, stop=None):
    nc = bass.Bass("TRN2", target_bir_lowering=False)

    def din(name, shape):
        return nc.dram_tensor(name, list(shape), F32, kind="ExternalInput").ap()

    xT = din("xT", [D, LTOK])
    vecs = din("vecs", [128, NV])
    ropeC = din("ropeC", [128, LTOK])
    ropeS = din("ropeS", [128, LTOK])
    cst = din("cst", [128, 128 + 512 + 512])
    ada_w = din("ada_w", [2, 2, D, 3 * D])
    kv_ada_w = din("kv_ada_w", [D, 2 * D])
    w_in = din("w_in", [D, 3 * D])
    w_out = din("w_out", [D, D])
    w_up = din("w_up", [2, D, DFF])
    w_down = din("w_down", [2, DFF, D])
    w_kdup = din("w_kdup", [D, 512])
    w_v = din("w_v", [D, 256])
    w_q = din("w_q", [D, D])
    w_o = din("w_o", [D, D])
    outT = nc.dram_tensor("outT", [D, max(1, n_tiles) * TM], F32, kind="ExternalOutput").ap()

    wsrc = {"w_in": w_in, "w_out": w_out, "up0": w_up[0], "down0": w_down[0], "wk": w_kdup, "wv": w_v,
            "wq": w_q, "wo": w_o, "up1": w_up[1], "down1": w_down[1]}
    worder = ["w_in", "w_out", "up0", "down0", "wk", "wv", "wq", "wo", "up1", "down1"]
    wscr = {k: nc.dram_tensor("scr_" + k, list(v.shape), BF16, kind="Internal").ap() for k, v in wsrc.items()}

    es = contextlib.ExitStack()
    with es:
        sc = Sched(nc, es)
        PE, ACT, DVE, POOL, SP = sc.pe, sc.act, sc.dve, sc.pool, sc.sp

        def sb(name, shape, dt):
            return es.enter_context(nc.sbuf_tensor(name, list(shape), dt))

        X = sb("X", [128, KC * TM], F32)
        XA = sb("XA", [128, 2 * KC * TM], BF16)
        BIG = sb("BIG", [128, 64 * TM + 128], BF16)
        WR = [sb(f"wr{i}", [128, 16, 256], BF16) for i in range(NSLOT)]
        KT = sb("KT", [128, 4, 2, 640], BF16)
        V = sb("V", [128, 5, 256], BF16)
        CT = sb("CT", [128, TM], F32)
        ST = sb("ST", [128, TM], F32)
        VEC = sb("VEC", [128, NV], F32)
        MOD = sb("MOD", [128, 4 * 48 + 32], F32)
        DER = sb("DER", [128, 16 * 16], F32)
        IDN = sb("IDN", [128, 128], BF16)
        MSK = sb("MSK", [128, 1024], BF16)
        ONES = sb("ONES", [128, 128], BF16)
        CACT = sb("CACT", [128, 16], BF16)
        RSTD = sb("RSTD", [128, TM], F32)
        TMP = [sb(f"tmp{i}", [128, TM], F32) for i in range(2)]
        TB = [sb(f"tb{i}", [128, TM], F32) for i in range(2)]
        SQ = [sb(f"sq{i}", [128, TM], BF16) for i in range(3)]
        ZT = sb("ZT", [128, 16, 2], F32)
        PF = [sb(f"pf{i}", [128, 512], F32) for i in range(2)]
        PN = [sb(f"pn{i}", [128, 512], BF16) for i in range(2)]
        PTS = [sb(f"pts{i}", [128, 512], BF16) for i in range(2)]
        SM = [sb(f"sm{i}", [128, 16], F32) for i in range(2)]
        SINK8 = sb("SINK8", [128, 32], F32)

        PS = [es.enter_context(nc.psum_tensor(f"ps{i}", [128, 512], F32)) for i in range(8)]

        rX = [Res() for _ in range(KC)]
        rXA = [Res() for _ in range(32)]
        rBIG = [Res() for _ in range(64)]
        rWR = [Res() for _ in range(NSLOT)]
        rPS = [Res() for _ in range(8)]
        rPTP = [Res(), Res()]
        rKT = Res()
        rV = [Res() for _ in range(5)]
        rCT = Res()
        rVEC = Res()
        rMOD = Res()
        rDER = Res()
        rCST = Res()
        rCACT = Res()
        rRSTD = Res()
        rTMP = [Res() for _ in range(3)]
        rTB = [Res() for _ in range(2)]
        rSQ = [Res() for _ in range(3)]
        rZT = Res()
        rPF = [Res(), Res()]
        rPN = [Res(), Res()]
        rPTS = [Res(), Res()]
        rSM = [Res(), Res()]
        rSINK = Res()

        wsem = [es.enter_context(nc.semaphore(f"wsem{i}")) for i in range(NSLOT)]
        wcnt = [[0] for _ in range(NSLOT)]
        whsem = [es.enter_context(nc.semaphore(f"whsem{i}")) for i in range(NSLOT)]
        whcnt = [[0] for _ in range(NSLOT)]
        pcsem = {k: es.enter_context(nc.semaphore("pc_" + k)) for k in worder}
        pccnt = {k: [0] for k in worder}
        rSCR = {k: Res() for k in worder}
        pcstate = {"n": 0, "look": 2}

        def precast_upto(idx):
            while pcstate["n"] <= min(idx, len(worder) - 1):
                k = worder[pcstate["n"]]
                pcstate["n"] += 1
                src, dst = wsrc[k], wscr[k]
                rows = src.shape[0]
                step = 256
                for r0 in range(0, rows, step):
                    sc.dma(POOL, dst[r0:r0 + step, :], src[r0:r0 + step, :], pcsem[k], pccnt[k], writes=[rSCR[k]])

        csem = es.enter_context(nc.semaphore("csem"))
        ccnt = [0]
        csem2 = es.enter_context(nc.semaphore("csem2"))
        ccnt2 = [0]
        xsemg = [es.enter_context(nc.semaphore(f"xsem{i}")) for i in range(4)]
        xcntg = [[0] for _ in range(4)]
        tsem = es.enter_context(nc.semaphore("tsem"))
        tcnt = [0]
        osem = es.enter_context(nc.semaphore("osem"))
        ocnt = [0]

        def vcol(name, j=0, w=1):
            o, _ = VOFF[name]
            return VEC[:, o + j:o + j + w]

        def x_ch(c, T):
            return X[:, c * T:(c + 1) * T]

        def xa_ch(which, c, T):
            base = which * KC * TM
            return XA[:, base + c * T: base + (c + 1) * T]

        def rxa(which, c):
            return rXA[which * 16 + c]

        def y_xa(c, T):
            v = XA[:, :].bitcast(F32)
            return v[:, c * T:(c + 1) * T]

        def ry_xa(c):
            return [rXA[2 * c], rXA[2 * c + 1]]

        def big_bf(g, T):
            return BIG[:, g * T:(g + 1) * T]

        def big_f32(j, T, base_g=0, extra=0):
            v = BIG[:, :].bitcast(F32)
            o = base_g * TM // 2
            return v[:, o + j * (T + extra): o + (j + 1) * (T + extra)]

        def rbig_f32(j, base_g=0):
            return [rBIG[base_g + 2 * j], rBIG[base_g + 2 * j + 1]]

        sc.dma(SP, VEC[:, :], vecs, csem, ccnt, writes=[rVEC])
        sc.dma(POOL, IDN[:, :], cst[:, 0:128], csem2, ccnt2, writes=[rCST])
        sc.dma(POOL, MSK[:, :], cst[:, 128:1152], csem2, ccnt2, writes=[rCST])
        sc.op(DVE, lambda: nc.vector.memset(ONES[:, :], 1.0), writes=[rCST])
        sc.op(DVE, lambda: nc.vector.memset(ZT[:, :, :], 0.0), writes=[rZT])
        sc.op(DVE, lambda: nc.vector.memset(KT[:, :, :, :], 0.0), writes=[rKT])
        sc.op(ACT, lambda: nc.scalar.activation(CACT[:, :], vcol("c", 0, 16), AF.Silu), reads=[rVEC], writes=[rCACT])
        sc.op(DVE, lambda: nc.vector.tensor_scalar(SINK8[:, :], vcol("sinks", 0, 32), 8.0, None, ALU.mult),
              reads=[rVEC], writes=[rSINK])

        wstate = {"n": 0}

        def load_slab(src, wkey=None):
            s = wstate["n"] % NSLOT
            wstate["n"] += 1
            kc, ncol = src.shape[1], src.shape[2]
            if wkey is None:
                sc.dma(POOL, WR[s][:, 0:kc, 0:ncol], src, wsem[s], wcnt[s], writes=[rWR[s]])
            else:
                sc.dma(SP, WR[s][:, 0:kc, 0:ncol], src, whsem[s], whcnt[s], reads=[rSCR[wkey]], writes=[rWR[s]])
            return WR[s], rWR[s]

        pstate = {"n": 0}

        def next_pset():
            s = pstate["n"] % 3
            pstate["n"] += 1
            return (2 * s, 2 * s + 1)

        def proj_fm(W, krows, ncols, in_aps, in_res, T, epi):
            wkey = None
            if isinstance(W, str):
                wkey = W
                precast_upto(worder.index(wkey) + pcstate["look"])
                W = wscr[wkey]
            KS = krows // 2048
            deferred = []
            for cg in range(ncols // 256):
                pset = next_pset()
                for ks in range(KS):
                    src = W[ks * 2048:(ks + 1) * 2048, cg * 256:(cg + 1) * 256].rearrange("(k p) c -> p k c", p=128)
                    slab, sres = load_slab(src, wkey)
                    fns = []
                    for j in range(2):
                        for ic in range(16):
                            kidx = ks * 16 + ic
                            fns.append(lambda j=j, ic=ic, kidx=kidx, slab=slab, pset=pset, ks=ks:
                                       nc.tensor.matmul(PS[pset[j]][:, 0:T], slab[:, ic, j * 128:(j + 1) * 128],
                                                        in_aps[kidx], start=(ks == 0 and ic == 0),
                                                        stop=(ks == KS - 1 and ic == 15)))
                    sc.group(PE, fns, reads=[sres] + in_res[ks * 16:(ks + 1) * 16],
                             writes=[rPS[pset[0]], rPS[pset[1]]])
                for d in deferred:
                    d()
                deferred = []
                for j in range(2):
                    r = epi(cg * 2 + j, PS[pset[j]][:, 0:T], rPS[pset[j]])
                    if r:
                        deferred.extend(r)
            for d in deferred:
                d()

        sqi = {"n": 0}

        def stats_mm(sq_ap, sq_res, T, first, last):
            sc.group(PE, [lambda: nc.tensor.matmul(PS[6][:, 0:T], ONES[:, :], sq_ap, start=first, stop=last)],
                     reads=[sq_res, rCST], writes=[rPS[6]])

        def finish_rstd(T):
            sc.op(ACT, lambda: nc.scalar.activation(RSTD[:, 0:T], PS[6][:, 0:T], AF.Sqrt, bias=EPS, scale=1.0 / D),
                  reads=[rPS[6]], writes=[rRSTD])
            sc.op(DVE, lambda: nc.vector.reciprocal(RSTD[:, 0:T], RSTD[:, 0:T]), reads=[rRSTD], writes=[rRSTD])

        def prenorm_stats(T):
            for c in range(KC):
                i = sqi["n"] % 3
                sqi["n"] += 1
                sc.op(ACT, lambda c=c, i=i: nc.scalar.activation(SQ[i][:, 0:T], x_ch(c, T), AF.Square),
                      reads=[rX[c]], writes=[rSQ[i]])
                stats_mm(SQ[i][:, 0:T], rSQ[i], T, c == 0, c == KC - 1)
            finish_rstd(T)

        tmi = {"n": 0}

        def modulate(which, T, acol, bcol):
            for c in range(KC):
                i = tmi["n"] % 2
                tmi["n"] += 1
                sc.op(DVE, lambda c=c, i=i: nc.vector.scalar_tensor_tensor(TMP[i][:, 0:T], x_ch(c, T), acol(c), RSTD[:, 0:T],
                                                                          ALU.mult, ALU.mult),
                      reads=[rX[c], rRSTD, rDER, rMOD], writes=[rTMP[i]])
                sc.op(ACT, lambda c=c, i=i: nc.scalar.activation(xa_ch(which, c, T), TMP[i][:, 0:T], AF.Identity,
                                                                bias=bcol(c), scale=1.0),
                      reads=[rTMP[i], rMOD, rDER], writes=[rxa(which, c)])

        def post_epi_factory(ydst, rydst, T, biascol=None):
            def epi(oc, ps, pres):
                if biascol is None:
                    sc.op(ACT, lambda: nc.scalar.copy(ydst(oc), ps), reads=[pres], writes=rydst(oc))
                else:
                    sc.op(ACT, lambda: nc.scalar.activation(ydst(oc), ps, AF.Identity, bias=biascol(oc), scale=1.0),
                          reads=[pres, rVEC], writes=rydst(oc))
                i = sqi["n"] % 3
                sqi["n"] += 1
                sc.op(DVE, lambda: nc.vector.tensor_tensor(SQ[i][:, 0:T], ydst(oc), ydst(oc), ALU.mult),
                      reads=rydst(oc), writes=[rSQ[i]])
                return [lambda: stats_mm(SQ[i][:, 0:T], rSQ[i], T, oc == 0, oc == KC - 1)]
            return epi

        def postnorm_apply(ysrc, rysrc, T, gcol, dst=None, rdst=None):
            finish_rstd(T)
            for c in range(KC):
                i = tmi["n"] % 2
                tmi["n"] += 1
                sc.op(DVE, lambda c=c, i=i: nc.vector.scalar_tensor_tensor(TMP[i][:, 0:T], ysrc(c), gcol(c), RSTD[:, 0:T],
                                                                          ALU.mult, ALU.mult),
                      reads=rysrc(c) + [rRSTD, rDER], writes=[rTMP[i]])
                if dst is None:
                    sc.op(DVE, lambda c=c, i=i: nc.vector.tensor_tensor(x_ch(c, T), x_ch(c, T), TMP[i][:, 0:T], ALU.add),
                          reads=[rTMP[i]], writes=[rX[c]])
                else:
                    sc.op(DVE, lambda c=c, i=i: nc.vector.tensor_tensor(dst(c), x_ch(c, T), TMP[i][:, 0:T], ALU.add),
                          reads=[rTMP[i], rX[c]], writes=rdst(c))

        cact_aps = [CACT[:, c:c + 1] for c in range(KC)]
        cact_res = [rCACT] * KC

        def ada_stage(W, ncols, modoff, bname):
            def epi(oc, ps, pres):
                sc.op(DVE, lambda: nc.vector.tensor_tensor(MOD[:, modoff + oc:modoff + oc + 1], ps, vcol(bname, oc, 1), ALU.add),
                      reads=[pres, rVEC], writes=[rMOD])
                return None
            proj_fm(W, 2048, ncols, cact_aps, cact_res, 1, epi)

        for l in range(2):
            for i in range(2):
                ada_stage(ada_w[l, i], 3 * D, (l * 2 + i) * 48, f"ada_b{l}{i}")
                if l == 0 and i == 0:
                    precast_upto(3)
        ada_stage(kv_ada_w, 2 * D, 192, "kv_ada_b")

        def der(j, w=16):
            return DER[:, j * 16:j * 16 + w]

        for l in range(2):
            for i in range(2):
                k = l * 2 + i
                mo = k * 48
                sc.op(DVE, lambda: nc.vector.tensor_scalar(der(2 * k), MOD[:, mo + 16:mo + 32], 1.0, None, ALU.add),
                      reads=[rMOD], writes=[rDER])
                sc.op(DVE, lambda: nc.vector.tensor_tensor(der(2 * k), der(2 * k), vcol(f"gpre{l}{i}", 0, 16), ALU.mult),
                      reads=[rDER, rVEC], writes=[rDER])
                sc.op(DVE, lambda: nc.vector.tensor_tensor(der(2 * k + 1), MOD[:, mo + 32:mo + 48], vcol(f"gpost{l}{i}", 0, 16), ALU.mult),
                      reads=[rMOD, rVEC], writes=[rDER])
        sc.op(DVE, lambda: nc.vector.tensor_scalar(der(8), MOD[:, 192 + 16:192 + 32], 1.0, None, ALU.add),
              reads=[rMOD], writes=[rDER])
        sc.op(DVE, lambda: nc.vector.tensor_tensor(der(8), der(8), vcol("kv_norm", 0, 16), ALU.mult),
              reads=[rDER, rVEC], writes=[rDER])

        if stop == "ada":
            sc.dma(POOL, outT[0:128, 0:224], MOD[:, :], osem, ocnt, reads=[rMOD, rDER])
            nc.gpsimd.wait_ge(osem, ocnt[0])
            return nc

        def Acol(l, i):
            return lambda c: DER[:, (2 * (l * 2 + i)) * 16 + c:(2 * (l * 2 + i)) * 16 + c + 1]

        def Gcol(l, i):
            return lambda c: DER[:, (2 * (l * 2 + i) + 1) * 16 + c:(2 * (l * 2 + i) + 1) * 16 + c + 1]

        def Bcol(l, i):
            return lambda c: MOD[:, (l * 2 + i) * 48 + c:(l * 2 + i) * 48 + c + 1]

        def Akv(c):
            return DER[:, 128 + c:128 + c + 1]

        def Bkv(c):
            return MOD[:, 192 + c:192 + c + 1]

        def mlp(l, T, final=False):
            prenorm_stats(T)
            modulate(0, T, Acol(l, 1), Bcol(l, 1))

            def up_epi(oc, ps, pres):
                i = tmi["n"] % 2
                tmi["n"] += 1
                sc.op(ACT, lambda: nc.scalar.activation(TMP[i][:, 0:T], ps, AF.Relu), reads=[pres], writes=[rTMP[i]])
                sc.op(DVE, lambda: nc.vector.tensor_tensor(big_bf(oc, T), TMP[i][:, 0:T], TMP[i][:, 0:T], ALU.mult),
                      reads=[rTMP[i]], writes=[rBIG[oc]])
                return None
            proj_fm(f"up{l}", 2048, DFF, [xa_ch(0, c, T) for c in range(KC)], [rxa(0, c) for c in range(KC)], T, up_epi)
            proj_fm(f"down{l}", DFF, D, [big_bf(g, T) for g in range(64)], [rBIG[g] for g in range(64)], T,
                    post_epi_factory(lambda oc: y_xa(oc, T), ry_xa, T))
            if final:
                postnorm_apply(lambda c: y_xa(c, T), ry_xa, T, Gcol(l, 1), dst=lambda c: big_f32(c, T), rdst=rbig_f32)
            else:
                postnorm_apply(lambda c: y_xa(c, T), ry_xa, T, Gcol(l, 1))

        def rope_chunk(src, rsrc, dst, rdst, T, c0, n):
            i = tmi["n"] % 2
            tmi["n"] += 1
            t2 = TMP[i]
            for q in range(4):
                qs = q ^ 1
                sc.op(DVE, lambda q=q, qs=qs: nc.vector.tensor_tensor(t2[q * 32:(q + 1) * 32, 0:n], src[qs * 32:(qs + 1) * 32, c0:c0 + n],
                                                                     ST[qs * 32:(qs + 1) * 32, c0:c0 + n], ALU.mult),
                      reads=[rsrc, rCT], writes=[rTMP[i]])
            sc.op(DVE, lambda: nc.vector.tensor_tensor(src[:, c0:c0 + n], src[:, c0:c0 + n], CT[:, c0:c0 + n], ALU.mult),
                  reads=[rsrc, rCT], writes=[rsrc])
            dl = dst if isinstance(dst, list) else [((0, 128), dst)]
            for (r0, r1), dap in dl:
                sc.op(DVE, lambda r0=r0, r1=r1, dap=dap: nc.vector.tensor_tensor(dap, src[r0:r1, c0:c0 + n], t2[r0:r1, 0:n], ALU.add),
                      reads=[rsrc, rTMP[i]], writes=rdst)

        tiles = [("halo", 0, HALO)] + [("main", HALO + j * TM, TM) for j in range(n_tiles)]
        tbi = {"n": 0}
        for tix, (kind, t0, T) in enumerate(tiles):
            is_halo = kind == "halo"
            pcstate["look"] = 0 if is_halo else 2
            for gq in range(4):
                sc.dma(POOL, X[:, gq * 4 * T:(gq + 1) * 4 * T].rearrange("p (c t) -> p c t", t=T),
                       xT[gq * 512:(gq + 1) * 512, t0:t0 + T].rearrange("(c p) t -> p c t", p=128), xsemg[gq], xcntg[gq],
                       writes=(rX if tix <= 1 else rX[gq * 4:(gq + 1) * 4]))
            sc.dma(POOL, CT[:, 0:T], ropeC[:, t0:t0 + T], tsem, tcnt, writes=[rCT])
            sc.dma(POOL, ST[:, 0:T], ropeS[:, t0:t0 + T], tsem, tcnt, writes=[rCT])

            prenorm_stats(T)
            modulate(0, T, Acol(0, 0), Bcol(0, 0))
            ZW = T + 2

            def zch(i):
                return big_f32(i, T, base_g=32, extra=2)

            def rz(i):
                return [rBIG[32 + 2 * i], rBIG[32 + 2 * i + 1], rBIG[min(63, 32 + 2 * i + 2)]]

            zall = BIG[:, :].bitcast(F32)[:, 16 * TM:16 * TM + 16 * ZW].rearrange("p (c t) -> p c t", t=ZW)
            rzall = [rBIG[g] for g in range(32, 64)]
            sc.op(DVE, lambda: nc.vector.tensor_copy(zall[:, :, 0:2], ZT[:, :, :]), reads=[rZT], writes=rzall)

            def conv_chunk(i):
                k = tmi["n"] % 2
                tmi["n"] += 1
                z = zch(i)
                sc.op(ACT, lambda: nc.scalar.activation(TMP[k][:, 0:T], z[:, 2:ZW], AF.Copy, scale=vcol("cw2", i, 1)),
                      reads=rz(i) + [rVEC], writes=[rTMP[k]])
                sc.op(DVE, lambda: nc.vector.scalar_tensor_tensor(TMP[k][:, 0:T], z[:, 1:ZW - 1], vcol("cw1", i, 1), TMP[k][:, 0:T],
                                                                 ALU.mult, ALU.add),
                      reads=rz(i) + [rVEC, rTMP[k]], writes=[rTMP[k]])
                sc.op(DVE, lambda: nc.vector.scalar_tensor_tensor(TMP[k][:, 0:T], z[:, 0:ZW - 2], vcol("cw0", i, 1), TMP[k][:, 0:T],
                                                                 ALU.mult, ALU.add),
                      reads=rz(i) + [rVEC, rTMP[k]], writes=[rTMP[k]])
                sc.op(DVE, lambda: nc.vector.tensor_tensor(xa_ch(1, i, T), TMP[k][:, 0:T], big_f32(i, T), ALU.mult),
                      reads=[rTMP[k]] + rbig_f32(i), writes=[rxa(1, i)])

            def win_epi(oc, ps, pres):
                if oc < 16:
                    sc.op(ACT, lambda: nc.scalar.copy(big_f32(oc, T), ps), reads=[pres], writes=rbig_f32(oc))
                elif oc < 32:
                    i = oc - 16
                    sc.op(ACT, lambda: nc.scalar.copy(zch(i)[:, 2:ZW], ps), reads=[pres], writes=rz(i))
                else:
                    i = oc - 32
                    sc.op(DVE, lambda: nc.vector.tensor_tensor(zch(i)[:, 2:ZW], zch(i)[:, 2:ZW], ps, ALU.mult),
                          reads=[pres] + rz(i), writes=rz(i))
                    conv_chunk(i)
                return None
            proj_fm("w_in", 2048, 3 * D, [xa_ch(0, c, T) for c in range(KC)], [rxa(0, c) for c in range(KC)], T, win_epi)

            if is_halo:
                sc.op(DVE, lambda: nc.vector.tensor_scalar(ZT[:, :, :], zall[:, :, T:T + 2], vcol("flag", 0, 1), None, ALU.mult),
                      reads=rzall + [rVEC], writes=[rZT])
            else:
                sc.op(DVE, lambda: nc.vector.tensor_copy(ZT[:, :, :], zall[:, :, T:T + 2]), reads=rzall, writes=[rZT])

            proj_fm("w_out", 2048, D, [xa_ch(1, c, T) for c in range(KC)], [rxa(1, c) for c in range(KC)], T,
                    post_epi_factory(lambda oc: big_f32(oc, T), rbig_f32, T))
            postnorm_apply(lambda c: big_f32(c, T), rbig_f32, T, Gcol(0, 0))

            mlp(0, T)

            if not do_l1:
                if is_halo and n_tiles == 0:
                    sc.dma(POOL, outT[:, 0:T].rearrange("(c p) t -> p c t", p=128),
                           X[:, 0:KC * T].rearrange("p (c t) -> p c t", t=T), osem, ocnt, reads=rX)
                if not is_halo:
                    j = tix - 1
                    sc.dma(POOL, outT[:, j * TM:(j + 1) * TM].rearrange("(c p) t -> p c t", p=128),
                           X[:, 0:KC * T].rearrange("p (c t) -> p c t", t=T), osem, ocnt, reads=rX)
                continue

            prenorm_stats(T)
            modulate(0, T, Akv, Bkv)
            if not is_halo:
                modulate(1, T, Acol(1, 0), Bcol(1, 0))

            def k_epi(g, ps, pres):
                i = tbi["n"] % 2
                tbi["n"] += 1
                sc.op(ACT, lambda: nc.scalar.activation(TB[i][:, 0:T], ps, AF.Identity, bias=vcol("bk", g, 1), scale=1.0),
                      reads=[pres, rVEC], writes=[rTB[i]])
                if is_halo:
                    rope_chunk(TB[i], rTB[i], [((0, 64), KT[0:64, g, 0, 0:128]), ((64, 128), KT[64:128, g, 1, 0:128])], [rKT], T, 2, 128)
                else:
                    rope_chunk(TB[i], rTB[i], [((0, 64), KT[0:64, g, 0, 128:640]), ((64, 128), KT[64:128, g, 1, 128:640])], [rKT], T, 0, T)
                return None
            proj_fm("wk", 2048, 512, [xa_ch(0, c, T) for c in range(KC)], [rxa(0, c) for c in range(KC)], T, k_epi)

            precast_upto(worder.index("wv") + pcstate["look"])
            vslab, vres = load_slab(wscr["wv"].rearrange("(k p) c -> p k c", p=128), "wv")
            nblk = 1 if is_halo else 4
            for tb in range(nblk):
                c0 = 2 if is_halo else tb * 128
                slot = 0 if is_halo else 1 + tb
                pb = 4 + (tb % 2)
                fns = [lambda ic=ic, c0=c0, pb=pb: nc.tensor.matmul(PS[pb][:, 0:256], xa_ch(0, ic, T)[:, c0:c0 + 128], vslab[:, ic, :],
                                                                   start=(ic == 0), stop=(ic == 15)) for ic in range(KC)]
                sc.group(PE, fns, reads=[vres] + [rxa(0, c) for c in range(KC)], writes=[rPS[pb]])
                sc.op(DVE, lambda slot=slot, pb=pb: nc.vector.tensor_tensor(V[:, slot, :], PS[pb][:, 0:256], vcol("bv", 0, 256), ALU.add),
                      reads=[rPS[pb], rVEC], writes=[rV[slot]])
            if is_halo:
                if n_tiles == 0:
                    sc.dma(POOL, outT[:, 0:T].rearrange("(c p) t -> p c t", p=128),
                           X[:, 0:KC * T].rearrange("p (c t) -> p c t", t=T), osem, ocnt, reads=rX)
                continue

            def q_epi(oc, ps, pres):
                i = tbi["n"] % 2
                tbi["n"] += 1
                sc.op(ACT, lambda: nc.scalar.activation(TB[i][:, 0:T], ps, AF.Identity, bias=vcol("bq", oc, 1), scale=1.0),
                      reads=[pres, rVEC], writes=[rTB[i]])
                rope_chunk(TB[i], rTB[i], big_bf(oc, T), [rBIG[oc]], T, 0, T)
                return None
            proj_fm("wq", 2048, D, [xa_ch(1, c, T) for c in range(KC)], [rxa(1, c) for c in range(KC)], T, q_epi)

            if stop == "q":
                sc.dma(POOL, outT[:, 0:T].rearrange("(c p) t -> p c t", p=128),
                       X[:, 0:KC * T].rearrange("p (c t) -> p c t", t=T), osem, ocnt, reads=rX + [rBIG[g_] for g_ in range(16)])
                nc.gpsimd.wait_ge(osem, ocnt[0])
                return nc
            def emit_S1(c, qb, a):
                g = c // 4
                sb_ = a
                first_blk = (tix == 1 and qb == 0)
                moff = 512 if first_blk else 0
                q0 = qb * 128
                fns = [
                    lambda: nc.tensor.matmul(PS[sb_][:, 0:512], IDN[:, :], MSK[:, moff:moff + 512], start=True, stop=False),
                    lambda: nc.tensor.matmul(PS[sb_][:, 0:256], big_bf(c, T)[:, q0:q0 + 128],
                                             KT[:, g, 0, q0:q0 + 256], start=False, stop=False),
                    lambda: nc.tensor.matmul(PS[sb_][:, 256:512], big_bf(c, T)[:, q0:q0 + 128],
                                             KT[:, g, 1, q0:q0 + 256], start=False, stop=True),
                ]
                sc.group(PE, fns, reads=[rCST, rBIG[c], rKT], writes=[rPS[sb_]])
                sm = SM[a]
                sc.op(DVE, lambda: nc.vector.reduce_max(sm[:, 0:2], PS[sb_][:, 0:512].rearrange("p (h k) -> p h k", h=2), AX.X),
                      reads=[rPS[sb_]], writes=[rSM[a]])
                sc.op(DVE, lambda: nc.vector.tensor_tensor(sm[:, 0:2], sm[:, 0:2], SINK8[:, 2 * c:2 * c + 2], ALU.max),
                      reads=[rSM[a], rSINK], writes=[rSM[a]])
                sc.op(DVE, lambda: nc.vector.tensor_scalar(sm[:, 2:4], sm[:, 0:2], -0.125, None, ALU.mult),
                      reads=[rSM[a]], writes=[rSM[a]])
                sc.op(DVE, lambda: nc.vector.memset(sm[:, 4:6], 0.0), writes=[rSM[a]])
                for hh in range(2):
                    sc.op(ACT, lambda hh=hh: nc.scalar.activation(PF[a][:, hh * 256:(hh + 1) * 256], PS[sb_][:, hh * 256:(hh + 1) * 256],
                                                                  AF.Exp, bias=sm[:, 2 + hh:3 + hh], scale=0.125,
                                                                  accum_out=sm[:, 4 + hh:5 + hh]),
                          reads=[rPS[sb_], rSM[a]], writes=[rPF[a], rSM[a]])
                    sc.op(ACT, lambda hh=hh: nc.scalar.activation(sm[:, 6 + hh:7 + hh], vcol("sinks", 2 * c + hh, 1), AF.Exp,
                                                                  bias=sm[:, 2 + hh:3 + hh], scale=1.0),
                          reads=[rSM[a], rVEC], writes=[rSM[a]])

            def emit_S2(c, qb, a):
                sm = SM[a]
                sc.op(DVE, lambda: nc.vector.tensor_tensor(sm[:, 8:10], sm[:, 4:6], sm[:, 6:8], ALU.add),
                      reads=[rSM[a]], writes=[rSM[a]])
                sc.op(DVE, lambda: nc.vector.reciprocal(sm[:, 8:10], sm[:, 8:10]), reads=[rSM[a]], writes=[rSM[a]])
                for hh in range(2):
                    sc.op(DVE, lambda hh=hh: nc.vector.tensor_scalar(PN[a][:, hh * 256:(hh + 1) * 256], PF[a][:, hh * 256:(hh + 1) * 256],
                                                                     sm[:, 8 + hh:9 + hh], None, ALU.mult),
                          reads=[rPF[a], rSM[a]], writes=[rPN[a]])

            def emit_T(c, qb, a):
                fns = [lambda k=k: nc.tensor.matmul(PS[4 + a][:, k * 128:(k + 1) * 128], PN[a][:, k * 128:(k + 1) * 128], IDN[:, :],
                                                    start=True, stop=True)
                       for k in range(4)]
                sc.group(PE, fns, reads=[rPN[a], rCST], writes=[rPS[4 + a]])
                sc.op(ACT, lambda: nc.scalar.copy(PTS[a][:, :], PS[4 + a][:, 0:512]), reads=[rPS[4 + a]], writes=[rPTS[a]])

            def emit_PV(c, qb, a):
                g = c // 4
                q0 = qb * 128
                ob = (2, 3) if c % 2 == 0 else (6, 7)
                fns = []
                for hh in range(2):
                    for kb in range(2):
                        vs = qb + kb
                        fns.append(lambda hh=hh, kb=kb, vs=vs:
                                   nc.tensor.matmul(PS[ob[hh]][0:64, q0:q0 + 128], V[:, vs, g * 64:(g + 1) * 64],
                                                    PTS[a][:, (hh * 2 + kb) * 128:(hh * 2 + kb + 1) * 128],
                                                    start=(kb == 0), stop=(kb == 1)))
                sc.group(PE, fns, reads=[rPTS[a], rV[qb], rV[qb + 1]], writes=[rPS[ob[0]], rPS[ob[1]]])
                if qb == 3:
                    sc.op(ACT, lambda: nc.scalar.copy(big_bf(16 + c, T)[0:64, :], PS[ob[0]][0:64, 0:T]), reads=[rPS[ob[0]]], writes=[rBIG[16 + c]])
                    sc.op(ACT, lambda: nc.scalar.copy(big_bf(16 + c, T)[64:128, :], PS[ob[1]][0:64, 0:T]), reads=[rPS[ob[1]]], writes=[rBIG[16 + c]])

            pairs = [(c, qb, (c * 4 + qb) % 2) for c in range(KC) for qb in range(4)]
            NP_ = len(pairs)
            for pi in range(NP_ + 3):
                if pi < NP_:
                    emit_S1(*pairs[pi])
                if 1 <= pi <= NP_:
                    emit_S2(*pairs[pi - 1])
                if 2 <= pi <= NP_ + 1:
                    emit_T(*pairs[pi - 2])
                if pi >= 3:
                    emit_PV(*pairs[pi - 3])

            if stop == "att":
                sc.dma(POOL, outT[:, 0:T].rearrange("(c p) t -> p c t", p=128),
                       X[:, 0:KC * T].rearrange("p (c t) -> p c t", t=T), osem, ocnt, reads=rX + [rBIG[g_] for g_ in range(32)])
                nc.gpsimd.wait_ge(osem, ocnt[0])
                return nc
            sc.op(DVE, lambda: nc.vector.tensor_copy(KT[:, :, :, 0:128], KT[:, :, :, 512:640]), reads=[rKT], writes=[rKT])
            sc.op(DVE, lambda: nc.vector.tensor_copy(V[:, 0, :], V[:, 4, :]), reads=[rV[4]], writes=[rV[0]])

            proj_fm("wo", 2048, D, [big_bf(16 + c, T) for c in range(KC)], [rBIG[16 + c] for c in range(KC)], T,
                    post_epi_factory(lambda oc: big_f32(oc, T, base_g=32), lambda oc: rbig_f32(oc, 32), T,
                                     biascol=lambda oc: vcol("bo", oc, 1)))
            postnorm_apply(lambda c: big_f32(c, T, base_g=32), lambda c: rbig_f32(c, 32), T, Gcol(1, 0))

            mlp(1, T, final=True)

            j = tix - 1
            obig = BIG[:, :].bitcast(F32)[:, 0:KC * T].rearrange("p (c t) -> p c t", t=T)
            sc.dma(POOL, outT[:, j * TM:(j + 1) * TM].rearrange("(c p) t -> p c t", p=128),
                   obig, osem, ocnt, reads=[rBIG[g_] for g_ in range(32)])

        nc.gpsimd.wait_ge(osem, ocnt[0])
    return nc


def _rope_tables(p0):
    inv = (np.float32(500000.0) ** (-np.arange(0, 16, 2, dtype=np.float32) / np.float32(16))).astype(np.float32)
    pos = (np.arange(LTOK, dtype=np.float32) + np.float32(p0 - HALO)).astype(np.float32)
    ang = (pos[:, None] * inv[None, :]).astype(np.float32)
    cos = np.cos(ang).astype(np.float32)
    sin = np.sin(ang).astype(np.float32)
    C = np.ones((128, LTOK), np.float32)
    Sg = np.zeros((128, LTOK), np.float32)
    for r in range(128):
        j = r % 64
        q = j // 32
        jj = j % 32
        if jj < 8:
            C[r] = cos[:, jj]
            Sg[r] = -sin[:, jj] if q == 0 else sin[:, jj]
    Ssw = np.ascontiguousarray(Sg.reshape(2, 2, 32, LTOK)[:, ::-1].reshape(128, LTOK))
    return C, Ssw


def _masks(first_core_half):
    i = np.arange(128)[:, None]
    j = np.arange(128)[None, :]
    mp = np.where(j > i, 0.0, NEG).astype(np.float32)
    mc = np.where(j <= i, 0.0, NEG).astype(np.float32)
    m2 = np.concatenate([mp, mc, mp, mc], axis=1)
    if first_core_half:
        mp0 = np.full((128, 128), NEG, np.float32)
    else:
        mp0 = mp
    m0 = np.concatenate([mp0, mc, mp0, mc], axis=1)
    return m2, m0


def _prep(inputs, n_cores=8):
    f = lambda a: np.ascontiguousarray(np.asarray(a, dtype=np.float32))
    x = f(inputs["x"])
    c = f(inputs["c"])
    ada_w = f(inputs["ada_w"])
    ada_b = f(inputs["ada_b"])
    norm_pre = f(inputs["norm_pre"])
    norm_post = f(inputs["norm_post"])
    conv_w = f(inputs["conv_w"])[0]
    w_kv = f(inputs["w_kv"])
    b_kv = f(inputs["b_kv"])
    w_q = f(inputs["w_q"])[0]
    b_q = f(inputs["b_q"])[0]
    sinks = f(inputs["sinks"])[0]
    b_o = f(inputs["b_o"])[0]

    qcols = np.concatenate([h * 64 + PERM for h in range(NH)])
    w_q_p = np.ascontiguousarray(w_q[:, qcols])
    b_q_p = b_q[qcols]
    kcols = np.concatenate([np.concatenate([g * 64 + PERM, g * 64 + PERM]) for g in range(4)])
    w_kdup = np.ascontiguousarray(w_kv[:, kcols])
    b_kdup = b_kv[kcols]
    w_v = np.ascontiguousarray(w_kv[:, 256:512])
    b_v = b_kv[256:512]

    shared = {
        "ada_w": ada_w, "kv_ada_w": f(inputs["kv_ada_w"]), "w_in": f(inputs["conv_w_in"])[0],
        "w_out": f(inputs["conv_w_out"])[0], "w_up": f(inputs["mlp_up"]), "w_down": f(inputs["mlp_down"]),
        "w_kdup": w_kdup, "w_v": w_v, "w_q": w_q_p, "w_o": f(inputs["w_o"])[0],
    }
    ident = np.eye(128, dtype=np.float32)
    in_maps = []
    for r in range(n_cores):
        b, hf = r // 2, r % 2
        p0 = hf * TOK
        xt = np.zeros((D, LTOK), np.float32)
        lo = p0 - HALO
        if lo >= 0:
            xt[:, :] = x[b, lo:p0 + TOK, :].T
        else:
            xt[:, HALO:] = x[b, 0:TOK, :].T
        vec = np.zeros((128, NV), np.float32)

        def put(name, arr):
            o, w = VOFF[name]
            assert arr.shape == (128, w), (name, arr.shape, w)
            vec[:, o:o + w] = arr
        put("c", _col(c[b]))
        for l in range(2):
            for i in range(2):
                put(f"ada_b{l}{i}", _col(ada_b[l, i]))
                put(f"gpre{l}{i}", _col(norm_pre[l, i]))
                put(f"gpost{l}{i}", _col(norm_post[l, i]))
        put("kv_ada_b", _col(inputs["kv_ada_b"]))
        put("kv_norm", _col(inputs["kv_norm"]))
        put("cw0", _col(conv_w[0]))
        put("cw1", _col(conv_w[1]))
        put("cw2", _col(conv_w[2]))
        put("bk", _col(b_kdup))
        put("bq", _col(b_q_p))
        put("bo", _col(b_o))
        put("sinks", np.broadcast_to(sinks[None, :], (128, 32)))
        put("flag", np.full((128, 1), float(hf), np.float32))
        put("bv", np.broadcast_to(b_v[None, :], (128, 256)))
        C, Sg = _rope_tables(p0)
        m2, m0 = _masks(hf == 0)
        cst = np.concatenate([ident, m2, m0], axis=1).astype(np.float32)
        m = {"xT": xt, "vecs": vec, "ropeC": C, "ropeS": Sg, "cst": cst}
        m.update(shared)
        in_maps.append(m)
    return in_maps


def kernel(**inputs):
    n = 8
    in_maps = _prep(inputs, n)
    nc = build_nc()
    res = run_bass_kernel_spmd(nc, in_maps, core_ids=list(range(n)))
    out = np.empty((NB, SEQ, D), np.float32)
    for r in range(n):
        b, hf = r // 2, r % 2
        out[b, hf * TOK:(hf + 1) * TOK, :] = res.results[r]["outT"].T
    return out
```

```python
import contextlib
import numpy as np
import concourse.bass as bass
import concourse.mybir as mybir
from concourse.bass_utils import run_bass_kernel_spmd

F32 = mybir.dt.float32
BF16 = mybir.dt.bfloat16
ALU = mybir.AluOpType
AF = mybir.ActivationFunctionType
AX = mybir.AxisListType

D = 2048
KC = 16
DFF = 8192
SEQ = 4096
NB = 4
NH = 32
HD = 64
HALO = 130
TM = 512
TOK = 2048
LTOK = TOK + HALO
EPS = 1e-6
NEG = -30000.0
NSLOT = 4

PERM = np.array(list(range(0, 8)) + list(range(16, 40)) + list(range(8, 16)) + list(range(40, 64)))


def _vec_layout():
    off = {}
    cur = 0

    def add(name, w):
        nonlocal cur
        off[name] = (cur, w)
        cur += w

    add("c", 16)
    for l in range(2):
        for i in range(2):
            add(f"ada_b{l}{i}", 48)
            add(f"gpre{l}{i}", 16)
            add(f"gpost{l}{i}", 16)
    add("kv_ada_b", 32)
    add("kv_norm", 16)
    add("cw0", 16)
    add("cw1", 16)
    add("cw2", 16)
    add("bk", 4)
    add("bq", 16)
    add("bo", 16)
    add("sinks", 32)
    add("flag", 1)
    add("bv", 256)
    return off, cur


VOFF, NV = _vec_layout()


def _col(v):
    return np.ascontiguousarray(np.asarray(v, np.float32).reshape(-1, 128).T)


class Res:
    __slots__ = ("w", "r")

    def __init__(self, init=None):
        self.w = dict(init) if init else {}
        self.r = {}


def _merge(d, t):
    for k, v in t.items():
        if d.get(k, 0) < v:
            d[k] = v


class Eng:
    def __init__(self, nc, es, h, name):
        self.h = h
        self.name = name
        self.sem = es.enter_context(nc.semaphore("e_" + name))
        self.cnt = 0
        self.seen = {}


class Sched:
    def __init__(self, nc, es):
        self.nc = nc
        self.pe = Eng(nc, es, nc.tensor, "pe")
        self.act = Eng(nc, es, nc.scalar, "act")
        self.dve = Eng(nc, es, nc.vector, "dve")
        self.pool = Eng(nc, es, nc.gpsimd, "pool")
        self.sp = Eng(nc, es, nc.sync, "sp")

    def _need(self, reads, writes):
        need = {}
        for r in reads:
            _merge(need, r.w)
        for w in writes:
            _merge(need, w.w)
            _merge(need, w.r)
        return need

    def _waits(self, eng, need):
        for sem, v in need.items():
            if eng is self.pe and sem is eng.sem:
                continue
            if eng.seen.get(sem, 0) >= v:
                continue
            eng.h.wait_ge(sem, v)
            eng.seen[sem] = v

    def _commit(self, tok, reads, writes):
        for r in reads:
            _merge(r.r, tok)
        for w in writes:
            _merge(w.w, tok)
            w.r = {}

    def op(self, eng, fn, reads=(), writes=()):
        self._waits(eng, self._need(reads, writes))
        ins = fn()
        eng.cnt += 1
        ins.then_inc(eng.sem, 1)
        tok = {eng.sem: eng.cnt}
        self._commit(tok, reads, writes)
        return tok

    def group(self, eng, fns, reads=(), writes=()):
        self._waits(eng, self._need(reads, writes))
        ins = None
        for f in fns:
            ins = f()
        eng.cnt += 1
        ins.then_inc(eng.sem, 1)
        tok = {eng.sem: eng.cnt}
        self._commit(tok, reads, writes)
        return tok

    def dma(self, eng, out, in_, sem, semcnt, reads=(), writes=()):
        self._waits(eng, self._need(reads, writes))
        ins = eng.h.dma_start(out=out, in_=in_)
        semcnt[0] += 16
        ins.then_inc(sem, 16)
        tok = {sem: semcnt[0]}
        self._commit(tok, reads, writes)
        return tok


def build_nc(n_tiles=4, do_l1=True, stop=None):
    nc = bass.Bass("TRN2", target_bir_lowering=False)

    def din(name, shape):
        return nc.dram_tensor(name, list(shape), F32, kind="ExternalInput").ap()

    xT = din("xT", [D, LTOK])
    vecs = din("vecs", [128, NV])
    ropeC = din("ropeC", [128, LTOK])
    ropeS = din("ropeS", [128, LTOK])
    cst = din("cst", [128, 128 + 512 + 512])
    ada_w = din("ada_w", [2, 2, D, 3 * D])
    kv_ada_w = din("kv_ada_w", [D, 2 * D])
    w_in = din("w_in", [D, 3 * D])
    w_out = din("w_out", [D, D])
    w_up = din("w_up", [2, D, DFF])
    w_down = din("w_down", [2, DFF, D])
    w_kdup = din("w_kdup", [D, 512])
    w_v = din("w_v", [D, 256])
    w_q = din("w_q", [D, D])
    w_o = din("w_o", [D, D])
    outT = nc.dram_tensor("outT", [D, max(1, n_tiles) * TM], F32, kind="ExternalOutput").ap()

    wsrc = {"w_in": w_in, "w_out": w_out, "up0": w_up[0], "down0": w_down[0], "wk": w_kdup, "wv": w_v,
            "wq": w_q, "wo": w_o, "up1": w_up[1], "down1": w_down[1]}
    worder = ["w_in", "w_out", "up0", "down0", "wk", "wv", "wq", "wo", "up1", "down1"]
    wscr = {k: nc.dram_tensor("scr_" + k, list(v.shape), BF16, kind="Internal").ap() for k, v in wsrc.items()}

    es = contextlib.ExitStack()
    with es:
        sc = Sched(nc, es)
        PE, ACT, DVE, POOL, SP = sc.pe, sc.act, sc.dve, sc.pool, sc.sp

        def sb(name, shape, dt):
            return es.enter_context(nc.sbuf_tensor(name, list(shape), dt))

        X = sb("X", [128, KC * TM], F32)
        XA = sb("XA", [128, 2 * KC * TM], BF16)
        BIG = sb("BIG", [128, 64 * TM + 128], BF16)
        WR = [sb(f"wr{i}", [128, 16, 256], BF16) for i in range(NSLOT)]
        KT = sb("KT", [128, 4, 2, 640], BF16)
        V = sb("V", [128, 5, 256], BF16)
        CT = sb("CT", [128, TM], F32)
        ST = sb("ST", [128, TM], F32)
        VEC = sb("VEC", [128, NV], F32)
        MOD = sb("MOD", [128, 4 * 48 + 32], F32)
        DER = sb("DER", [128, 16 * 16], F32)
        IDN = sb("IDN", [128, 128], BF16)
        MSK = sb("MSK", [128, 1024], BF16)
        ONES = sb("ONES", [128, 128], BF16)
        CACT = sb("CACT", [128, 16], BF16)
        RSTD = sb("RSTD", [128, TM], F32)
        TMP = [sb(f"tmp{i}", [128, TM], F32) for i in range(2)]
        TB = [sb(f"tb{i}", [128, TM], F32) for i in range(2)]
        SQ = [sb(f"sq{i}", [128, TM], BF16) for i in range(3)]
        ZT = sb("ZT", [128, 16, 2], F32)
        PF = [sb(f"pf{i}", [128, 512], F32) for i in range(2)]
        PN = [sb(f"pn{i}", [128, 512], BF16) for i in range(2)]
        PTS = [sb(f"pts{i}", [128, 512], BF16) for i in range(2)]
        SM = [sb(f"sm{i}", [128, 16], F32) for i in range(2)]
        SINK8 = sb("SINK8", [128, 32], F32)

        PS = [es.enter_context(nc.psum_tensor(f"ps{i}", [128, 512], F32)) for i in range(8)]

        rX = [Res() for _ in range(KC)]
        rXA = [Res() for _ in range(32)]
        rBIG = [Res() for _ in range(64)]
        rWR = [Res() for _ in range(NSLOT)]
        rPS = [Res() for _ in range(8)]
        rPTP = [Res(), Res()]
        rKT = Res()
        rV = [Res() for _ in range(5)]
        rCT = Res()
        rVEC = Res()
        rMOD = Res()
        rDER = Res()
        rCST = Res()
        rCACT = Res()
        rRSTD = Res()
        rTMP = [Res() for _ in range(3)]
        rTB = [Res() for _ in range(2)]
        rSQ = [Res() for _ in range(3)]
        rZT = Res()
        rPF = [Res(), Res()]
        rPN = [Res(), Res()]
        rPTS = [Res(), Res()]
        rSM = [Res(), Res()]
        rSINK = Res()

        wsem = [es.enter_context(nc.semaphore(f"wsem{i}")) for i in range(NSLOT)]
        wcnt = [[0] for _ in range(NSLOT)]
        whsem = [es.enter_context(nc.semaphore(f"whsem{i}")) for i in range(NSLOT)]
        whcnt = [[0] for _ in range(NSLOT)]
        pcsem = {k: es.enter_context(nc.semaphore("pc_" + k)) for k in worder}
        pccnt = {k: [0] for k in worder}
        rSCR = {k: Res() for k in worder}
        pcstate = {"n": 0, "look": 2}

        def precast_upto(idx):
            while pcstate["n"] <= min(idx, len(worder) - 1):
                k = worder[pcstate["n"]]
                pcstate["n"] += 1
                src, dst = wsrc[k], wscr[k]
                rows = src.shape[0]
                step = 256
                for r0 in range(0, rows, step):
                    sc.dma(POOL, dst[r0:r0 + step, :], src[r0:r0 + step, :], pcsem[k], pccnt[k], writes=[rSCR[k]])

        csem = es.enter_context(nc.semaphore("csem"))
        ccnt = [0]
        csem2 = es.enter_context(nc.semaphore("csem2"))
        ccnt2 = [0]
        xsemg = [es.enter_context(nc.semaphore(f"xsem{i}")) for i in range(4)]
        xcntg = [[0] for _ in range(4)]
        tsem = es.enter_context(nc.semaphore("tsem"))
        tcnt = [0]
        osem = es.enter_context(nc.semaphore("osem"))
        ocnt = [0]

        def vcol(name, j=0, w=1):
            o, _ = VOFF[name]
            return VEC[:, o + j:o + j + w]

        def x_ch(c, T):
            return X[:, c * T:(c + 1) * T]

        def xa_ch(which, c, T):
            base = which * KC * TM
            return XA[:, base + c * T: base + (c + 1) * T]

        def rxa(which, c):
            return rXA[which * 16 + c]

        def y_xa(c, T):
            v = XA[:, :].bitcast(F32)
            return v[:, c * T:(c + 1) * T]

        def ry_xa(c):
            return [rXA[2 * c], rXA[2 * c + 1]]

        def big_bf(g, T):
            return BIG[:, g * T:(g + 1) * T]

        def big_f32(j, T, base_g=0, extra=0):
            v = BIG[:, :].bitcast(F32)
            o = base_g * TM // 2
            return v[:, o + j * (T + extra): o + (j + 1) * (T + extra)]

        def rbig_f32(j, base_g=0):
            return [rBIG[base_g + 2 * j], rBIG[base_g + 2 * j + 1]]

        sc.dma(SP, VEC[:, :], vecs, csem, ccnt, writes=[rVEC])
        sc.dma(POOL, IDN[:, :], cst[:, 0:128], csem2, ccnt2, writes=[rCST])
        sc.dma(POOL, MSK[:, :], cst[:, 128:1152], csem2, ccnt2, writes=[rCST])
        sc.op(DVE, lambda: nc.vector.memset(ONES[:, :], 1.0), writes=[rCST])
        sc.op(DVE, lambda: nc.vector.memset(ZT[:, :, :], 0.0), writes=[rZT])
        sc.op(DVE, lambda: nc.vector.memset(KT[:, :, :, :], 0.0), writes=[rKT])
        sc.op(ACT, lambda: nc.scalar.activation(CACT[:, :], vcol("c", 0, 16), AF.Silu), reads=[rVEC], writes=[rCACT])
        sc.op(DVE, lambda: nc.vector.tensor_scalar(SINK8[:, :], vcol("sinks", 0, 32), 8.0, None, ALU.mult),
              reads=[rVEC], writes=[rSINK])

        wstate = {"n": 0}

        def load_slab(src, wkey=None):
            s = wstate["n"] % NSLOT
            wstate["n"] += 1
            kc, ncol = src.shape[1], src.shape[2]
            if wkey is None:
                sc.dma(POOL, WR[s][:, 0:kc, 0:ncol], src, wsem[s], wcnt[s], writes=[rWR[s]])
            else:
                sc.dma(SP, WR[s][:, 0:kc, 0:ncol], src, whsem[s], whcnt[s], reads=[rSCR[wkey]], writes=[rWR[s]])
            return WR[s], rWR[s]

        pstate = {"n": 0}

        def next_pset():
            s = pstate["n"] % 3
            pstate["n"] += 1
            return (2 * s, 2 * s + 1)

        def proj_fm(W, krows, ncols, in_aps, in_res, T, epi):
            wkey = None
            if isinstance(W, str):
                wkey = W
                precast_upto(worder.index(wkey) + pcstate["look"])
                W = wscr[wkey]
            KS = krows // 2048
            deferred = []
            for cg in range(ncols // 256):
                pset = next_pset()
                for ks in range(KS):
                    src = W[ks * 2048:(ks + 1) * 2048, cg * 256:(cg + 1) * 256].rearrange("(k p) c -> p k c", p=128)
                    slab, sres = load_slab(src, wkey)
                    fns = []
                    for j in range(2):
                        for ic in range(16):
                            kidx = ks * 16 + ic
                            fns.append(lambda j=j, ic=ic, kidx=kidx, slab=slab, pset=pset, ks=ks:
                                       nc.tensor.matmul(PS[pset[j]][:, 0:T], slab[:, ic, j * 128:(j + 1) * 128],
                                                        in_aps[kidx], start=(ks == 0 and ic == 0),
                                                        stop=(ks == KS - 1 and ic == 15)))
                    if cg == 0 and ks == 0 and T > 1:
                        for fi, f in enumerate(fns):
                            sc.group(PE, [f], reads=[sres, in_res[fi % 16]], writes=[rPS[pset[0]], rPS[pset[1]]])
                    else:
                        sc.group(PE, fns, reads=[sres] + in_res[ks * 16:(ks + 1) * 16],
                                 writes=[rPS[pset[0]], rPS[pset[1]]])
                for d in deferred:
                    d()
                deferred = []
                for j in range(2):
                    r = epi(cg * 2 + j, PS[pset[j]][:, 0:T], rPS[pset[j]])
                    if r:
                        deferred.extend(r)
            for d in deferred:
                d()

        sqi = {"n": 0}

        def stats_mm(sq_ap, sq_res, T, first, last):
            sc.group(PE, [lambda: nc.tensor.matmul(PS[6][:, 0:T], ONES[:, :], sq_ap, start=first, stop=last)],
                     reads=[sq_res, rCST], writes=[rPS[6]])

        def finish_rstd(T):
            sc.op(ACT, lambda: nc.scalar.activation(RSTD[:, 0:T], PS[6][:, 0:T], AF.Sqrt, bias=EPS, scale=1.0 / D),
                  reads=[rPS[6]], writes=[rRSTD])
            sc.op(DVE, lambda: nc.vector.reciprocal(RSTD[:, 0:T], RSTD[:, 0:T]), reads=[rRSTD], writes=[rRSTD])

        def prenorm_stats(T):
            for c in range(KC):
                i = sqi["n"] % 3
                sqi["n"] += 1
                if c % 2 == 0:
                    sc.op(ACT, lambda c=c, i=i: nc.scalar.activation(SQ[i][:, 0:T], x_ch(c, T), AF.Square),
                          reads=[rX[c]], writes=[rSQ[i]])
                else:
                    sc.op(DVE, lambda c=c, i=i: nc.vector.tensor_tensor(SQ[i][:, 0:T], x_ch(c, T), x_ch(c, T), ALU.mult),
                          reads=[rX[c]], writes=[rSQ[i]])
                stats_mm(SQ[i][:, 0:T], rSQ[i], T, c == 0, c == KC - 1)
            finish_rstd(T)

        tmi = {"n": 0}

        def modulate(which, T, acol, bcol):
            for c in range(KC):
                i = tmi["n"] % 2
                tmi["n"] += 1
                sc.op(DVE, lambda c=c, i=i: nc.vector.scalar_tensor_tensor(TMP[i][:, 0:T], x_ch(c, T), acol(c), RSTD[:, 0:T],
                                                                          ALU.mult, ALU.mult),
                      reads=[rX[c], rRSTD, rDER, rMOD], writes=[rTMP[i]])
                sc.op(ACT, lambda c=c, i=i: nc.scalar.activation(xa_ch(which, c, T), TMP[i][:, 0:T], AF.Identity,
                                                                bias=bcol(c), scale=1.0),
                      reads=[rTMP[i], rMOD, rDER], writes=[rxa(which, c)])

        def post_epi_factory(ydst, rydst, T, biascol=None):
            def epi(oc, ps, pres):
                if biascol is None:
                    sc.op(ACT, lambda: nc.scalar.copy(ydst(oc), ps), reads=[pres], writes=rydst(oc))
                else:
                    sc.op(ACT, lambda: nc.scalar.activation(ydst(oc), ps, AF.Identity, bias=biascol(oc), scale=1.0),
                          reads=[pres, rVEC], writes=rydst(oc))
                i = sqi["n"] % 3
                sqi["n"] += 1
                sc.op(DVE, lambda: nc.vector.tensor_tensor(SQ[i][:, 0:T], ydst(oc), ydst(oc), ALU.mult),
                      reads=rydst(oc), writes=[rSQ[i]])
                return [lambda: stats_mm(SQ[i][:, 0:T], rSQ[i], T, oc == 0, oc == KC - 1)]
            return epi

        def postnorm_apply(ysrc, rysrc, T, gcol, dst=None, rdst=None):
            finish_rstd(T)
            for c in range(KC):
                i = tmi["n"] % 2
                tmi["n"] += 1
                sc.op(DVE, lambda c=c, i=i: nc.vector.scalar_tensor_tensor(TMP[i][:, 0:T], ysrc(c), gcol(c), RSTD[:, 0:T],
                                                                          ALU.mult, ALU.mult),
                      reads=rysrc(c) + [rRSTD, rDER], writes=[rTMP[i]])
                if dst is None:
                    sc.op(DVE, lambda c=c, i=i: nc.vector.tensor_tensor(x_ch(c, T), x_ch(c, T), TMP[i][:, 0:T], ALU.add),
                          reads=[rTMP[i]], writes=[rX[c]])
                else:
                    sc.op(DVE, lambda c=c, i=i: nc.vector.tensor_tensor(dst(c), x_ch(c, T), TMP[i][:, 0:T], ALU.add),
                          reads=[rTMP[i], rX[c]], writes=rdst(c))

        cact_aps = [CACT[:, c:c + 1] for c in range(KC)]
        cact_res = [rCACT] * KC

        def ada_stage(W, ncols, modoff, bname):
            def epi(oc, ps, pres):
                sc.op(DVE, lambda: nc.vector.tensor_tensor(MOD[:, modoff + oc:modoff + oc + 1], ps, vcol(bname, oc, 1), ALU.add),
                      reads=[pres, rVEC], writes=[rMOD])
                return None
            proj_fm(W, 2048, ncols, cact_aps, cact_res, 1, epi)

        for l in range(2):
            for i in range(2):
                ada_stage(ada_w[l, i], 3 * D, (l * 2 + i) * 48, f"ada_b{l}{i}")
                if l == 0 and i == 0:
                    precast_upto(3)
        ada_stage(kv_ada_w, 2 * D, 192, "kv_ada_b")

        def der(j, w=16):
            return DER[:, j * 16:j * 16 + w]

        for l in range(2):
            for i in range(2):
                k = l * 2 + i
                mo = k * 48
                sc.op(DVE, lambda: nc.vector.tensor_scalar(der(2 * k), MOD[:, mo + 16:mo + 32], 1.0, None, ALU.add),
                      reads=[rMOD], writes=[rDER])
                sc.op(DVE, lambda: nc.vector.tensor_tensor(der(2 * k), der(2 * k), vcol(f"gpre{l}{i}", 0, 16), ALU.mult),
                      reads=[rDER, rVEC], writes=[rDER])
                sc.op(DVE, lambda: nc.vector.tensor_tensor(der(2 * k + 1), MOD[:, mo + 32:mo + 48], vcol(f"gpost{l}{i}", 0, 16), ALU.mult),
                      reads=[rMOD, rVEC], writes=[rDER])
        sc.op(DVE, lambda: nc.vector.tensor_scalar(der(8), MOD[:, 192 + 16:192 + 32], 1.0, None, ALU.add),
              reads=[rMOD], writes=[rDER])
        sc.op(DVE, lambda: nc.vector.tensor_tensor(der(8), der(8), vcol("kv_norm", 0, 16), ALU.mult),
              reads=[rDER, rVEC], writes=[rDER])

        if stop == "ada":
            sc.dma(POOL, outT[0:128, 0:224], MOD[:, :], osem, ocnt, reads=[rMOD, rDER])
            nc.gpsimd.wait_ge(osem, ocnt[0])
            return nc

        def Acol(l, i):
            return lambda c: DER[:, (2 * (l * 2 + i)) * 16 + c:(2 * (l * 2 + i)) * 16 + c + 1]

        def Gcol(l, i):
            return lambda c: DER[:, (2 * (l * 2 + i) + 1) * 16 + c:(2 * (l * 2 + i) + 1) * 16 + c + 1]

        def Bcol(l, i):
            return lambda c: MOD[:, (l * 2 + i) * 48 + c:(l * 2 + i) * 48 + c + 1]

        def Akv(c):
            return DER[:, 128 + c:128 + c + 1]

        def Bkv(c):
            return MOD[:, 192 + c:192 + c + 1]

        def mlp(l, T, final=False):
            prenorm_stats(T)
            modulate(0, T, Acol(l, 1), Bcol(l, 1))

            def up_epi(oc, ps, pres):
                i = tmi["n"] % 2
                tmi["n"] += 1
                sc.op(ACT, lambda: nc.scalar.activation(TMP[i][:, 0:T], ps, AF.Relu), reads=[pres], writes=[rTMP[i]])
                sc.op(DVE, lambda: nc.vector.tensor_tensor(big_bf(oc, T), TMP[i][:, 0:T], TMP[i][:, 0:T], ALU.mult),
                      reads=[rTMP[i]], writes=[rBIG[oc]])
                return None
            proj_fm(f"up{l}", 2048, DFF, [xa_ch(0, c, T) for c in range(KC)], [rxa(0, c) for c in range(KC)], T, up_epi)
            proj_fm(f"down{l}", DFF, D, [big_bf(g, T) for g in range(64)], [rBIG[g] for g in range(64)], T,
                    post_epi_factory(lambda oc: y_xa(oc, T), ry_xa, T))
            if final:
                postnorm_apply(lambda c: y_xa(c, T), ry_xa, T, Gcol(l, 1), dst=lambda c: big_f32(c, T), rdst=rbig_f32)
            else:
                postnorm_apply(lambda c: y_xa(c, T), ry_xa, T, Gcol(l, 1))

        def rope_chunk(src, rsrc, dst, rdst, T, c0, n):
            i = tmi["n"] % 2
            tmi["n"] += 1
            t2 = TMP[i]
            for q in range(4):
                qs = q ^ 1
                sc.op(DVE, lambda q=q, qs=qs: nc.vector.tensor_tensor(t2[q * 32:(q + 1) * 32, 0:n], src[qs * 32:(qs + 1) * 32, c0:c0 + n],
                                                                     ST[qs * 32:(qs + 1) * 32, c0:c0 + n], ALU.mult),
                      reads=[rsrc, rCT], writes=[rTMP[i]])
            sc.op(DVE, lambda: nc.vector.tensor_tensor(src[:, c0:c0 + n], src[:, c0:c0 + n], CT[:, c0:c0 + n], ALU.mult),
                  reads=[rsrc, rCT], writes=[rsrc])
            dl = dst if isinstance(dst, list) else [((0, 128), dst)]
            for (r0, r1), dap in dl:
                sc.op(DVE, lambda r0=r0, r1=r1, dap=dap: nc.vector.tensor_tensor(dap, src[r0:r1, c0:c0 + n], t2[r0:r1, 0:n], ALU.add),
                      reads=[rsrc, rTMP[i]], writes=rdst)

        tiles = [("halo", 0, HALO)] + [("main", HALO + j * TM, TM) for j in range(n_tiles)]
        tbi = {"n": 0}
        for tix, (kind, t0, T) in enumerate(tiles):
            is_halo = kind == "halo"
            pcstate["look"] = 0 if is_halo else 2
            for gq in range(4):
                sc.dma(POOL, X[:, gq * 4 * T:(gq + 1) * 4 * T].rearrange("p (c t) -> p c t", t=T),
                       xT[gq * 512:(gq + 1) * 512, t0:t0 + T].rearrange("(c p) t -> p c t", p=128), xsemg[gq], xcntg[gq],
                       writes=(rX if tix <= 1 else rX[gq * 4:(gq + 1) * 4]))
            sc.dma(POOL, CT[:, 0:T], ropeC[:, t0:t0 + T], tsem, tcnt, writes=[rCT])
            sc.dma(POOL, ST[:, 0:T], ropeS[:, t0:t0 + T], tsem, tcnt, writes=[rCT])

            prenorm_stats(T)
            modulate(0, T, Acol(0, 0), Bcol(0, 0))
            ZW = T + 2

            def zch(i):
                return big_f32(i, T, base_g=32, extra=2)

            def rz(i):
                return [rBIG[32 + 2 * i], rBIG[32 + 2 * i + 1], rBIG[min(63, 32 + 2 * i + 2)]]

            zall = BIG[:, :].bitcast(F32)[:, 16 * TM:16 * TM + 16 * ZW].rearrange("p (c t) -> p c t", t=ZW)
            rzall = [rBIG[g] for g in range(32, 64)]
            sc.op(DVE, lambda: nc.vector.tensor_copy(zall[:, :, 0:2], ZT[:, :, :]), reads=[rZT], writes=rzall)

            def conv_chunk(i):
                k = tmi["n"] % 2
                tmi["n"] += 1
                z = zch(i)
                sc.op(ACT, lambda: nc.scalar.activation(TMP[k][:, 0:T], z[:, 2:ZW], AF.Copy, scale=vcol("cw2", i, 1)),
                      reads=rz(i) + [rVEC], writes=[rTMP[k]])
                sc.op(DVE, lambda: nc.vector.scalar_tensor_tensor(TMP[k][:, 0:T], z[:, 1:ZW - 1], vcol("cw1", i, 1), TMP[k][:, 0:T],
                                                                 ALU.mult, ALU.add),
                      reads=rz(i) + [rVEC, rTMP[k]], writes=[rTMP[k]])
                sc.op(DVE, lambda: nc.vector.scalar_tensor_tensor(TMP[k][:, 0:T], z[:, 0:ZW - 2], vcol("cw0", i, 1), TMP[k][:, 0:T],
                                                                 ALU.mult, ALU.add),
                      reads=rz(i) + [rVEC, rTMP[k]], writes=[rTMP[k]])
                sc.op(DVE, lambda: nc.vector.tensor_tensor(xa_ch(1, i, T), TMP[k][:, 0:T], big_f32(i, T), ALU.mult),
                      reads=[rTMP[k]] + rbig_f32(i), writes=[rxa(1, i)])

            def win_epi(oc, ps, pres):
                if oc < 16:
                    sc.op(ACT, lambda: nc.scalar.copy(big_f32(oc, T), ps), reads=[pres], writes=rbig_f32(oc))
                elif oc < 32:
                    i = oc - 16
                    sc.op(ACT, lambda: nc.scalar.copy(zch(i)[:, 2:ZW], ps), reads=[pres], writes=rz(i))
                else:
                    i = oc - 32
                    sc.op(DVE, lambda: nc.vector.tensor_tensor(zch(i)[:, 2:ZW], zch(i)[:, 2:ZW], ps, ALU.mult),
                          reads=[pres] + rz(i), writes=rz(i))
                    conv_chunk(i)
                return None
            proj_fm("w_in", 2048, 3 * D, [xa_ch(0, c, T) for c in range(KC)], [rxa(0, c) for c in range(KC)], T, win_epi)

            if is_halo:
                sc.op(DVE, lambda: nc.vector.tensor_scalar(ZT[:, :, :], zall[:, :, T:T + 2], vcol("flag", 0, 1), None, ALU.mult),
                      reads=rzall + [rVEC], writes=[rZT])
            else:
                sc.op(DVE, lambda: nc.vector.tensor_copy(ZT[:, :, :], zall[:, :, T:T + 2]), reads=rzall, writes=[rZT])

            proj_fm("w_out", 2048, D, [xa_ch(1, c, T) for c in range(KC)], [rxa(1, c) for c in range(KC)], T,
                    post_epi_factory(lambda oc: big_f32(oc, T), rbig_f32, T))
            postnorm_apply(lambda c: big_f32(c, T), rbig_f32, T, Gcol(0, 0))

            mlp(0, T)

            if not do_l1:
                if is_halo and n_tiles == 0:
                    sc.dma(POOL, outT[:, 0:T].rearrange("(c p) t -> p c t", p=128),
                           X[:, 0:KC * T].rearrange("p (c t) -> p c t", t=T), osem, ocnt, reads=rX)
                if not is_halo:
                    j = tix - 1
                    sc.dma(POOL, outT[:, j * TM:(j + 1) * TM].rearrange("(c p) t -> p c t", p=128),
                           X[:, 0:KC * T].rearrange("p (c t) -> p c t", t=T), osem, ocnt, reads=rX)
                continue

            prenorm_stats(T)
            modulate(0, T, Akv, Bkv)
            if not is_halo:
                modulate(1, T, Acol(1, 0), Bcol(1, 0))

            def k_epi(g, ps, pres):
                i = tbi["n"] % 2
                tbi["n"] += 1
                sc.op(ACT, lambda: nc.scalar.activation(TB[i][:, 0:T], ps, AF.Identity, bias=vcol("bk", g, 1), scale=1.0),
                      reads=[pres, rVEC], writes=[rTB[i]])
                if is_halo:
                    rope_chunk(TB[i], rTB[i], [((0, 64), KT[0:64, g, 0, 0:128]), ((64, 128), KT[64:128, g, 1, 0:128])], [rKT], T, 2, 128)
                else:
                    rope_chunk(TB[i], rTB[i], [((0, 64), KT[0:64, g, 0, 128:640]), ((64, 128), KT[64:128, g, 1, 128:640])], [rKT], T, 0, T)
                return None
            proj_fm("wk", 2048, 512, [xa_ch(0, c, T) for c in range(KC)], [rxa(0, c) for c in range(KC)], T, k_epi)

            precast_upto(worder.index("wv") + pcstate["look"])
            vslab, vres = load_slab(wscr["wv"].rearrange("(k p) c -> p k c", p=128), "wv")
            nblk = 1 if is_halo else 4
            for tb in range(nblk):
                c0 = 2 if is_halo else tb * 128
                slot = 0 if is_halo else 1 + tb
                pb = 4 + (tb % 2)
                fns = [lambda ic=ic, c0=c0, pb=pb: nc.tensor.matmul(PS[pb][:, 0:256], xa_ch(0, ic, T)[:, c0:c0 + 128], vslab[:, ic, :],
                                                                   start=(ic == 0), stop=(ic == 15)) for ic in range(KC)]
                sc.group(PE, fns, reads=[vres] + [rxa(0, c) for c in range(KC)], writes=[rPS[pb]])
                sc.op(DVE, lambda slot=slot, pb=pb: nc.vector.tensor_tensor(V[:, slot, :], PS[pb][:, 0:256], vcol("bv", 0, 256), ALU.add),
                      reads=[rPS[pb], rVEC], writes=[rV[slot]])
            if is_halo:
                if n_tiles == 0:
                    sc.dma(POOL, outT[:, 0:T].rearrange("(c p) t -> p c t", p=128),
                           X[:, 0:KC * T].rearrange("p (c t) -> p c t", t=T), osem, ocnt, reads=rX)
                continue

            def q_epi(oc, ps, pres):
                i = tbi["n"] % 2
                tbi["n"] += 1
                sc.op(ACT, lambda: nc.scalar.activation(TB[i][:, 0:T], ps, AF.Identity, bias=vcol("bq", oc, 1), scale=1.0),
                      reads=[pres, rVEC], writes=[rTB[i]])
                rope_chunk(TB[i], rTB[i], big_bf(oc, T), [rBIG[oc]], T, 0, T)
                return None
            proj_fm("wq", 2048, D, [xa_ch(1, c, T) for c in range(KC)], [rxa(1, c) for c in range(KC)], T, q_epi)

            if stop == "q":
                sc.dma(POOL, outT[:, 0:T].rearrange("(c p) t -> p c t", p=128),
                       X[:, 0:KC * T].rearrange("p (c t) -> p c t", t=T), osem, ocnt, reads=rX + [rBIG[g_] for g_ in range(16)])
                nc.gpsimd.wait_ge(osem, ocnt[0])
                return nc
            def emit_S1(c, qb, a):
                g = c // 4
                sb_ = a
                first_blk = (tix == 1 and qb == 0)
                moff = 512 if first_blk else 0
                q0 = qb * 128
                fns = [
                    lambda: nc.tensor.matmul(PS[sb_][:, 0:512], IDN[:, :], MSK[:, moff:moff + 512], start=True, stop=False),
                    lambda: nc.tensor.matmul(PS[sb_][:, 0:256], big_bf(c, T)[:, q0:q0 + 128],
                                             KT[:, g, 0, q0:q0 + 256], start=False, stop=False),
                    lambda: nc.tensor.matmul(PS[sb_][:, 256:512], big_bf(c, T)[:, q0:q0 + 128],
                                             KT[:, g, 1, q0:q0 + 256], start=False, stop=True),
                ]
                sc.group(PE, fns, reads=[rCST, rBIG[c], rKT], writes=[rPS[sb_]])
                sm = SM[a]
                sc.op(DVE, lambda: nc.vector.reduce_max(sm[:, 0:2], PS[sb_][:, 0:512].rearrange("p (h k) -> p h k", h=2), AX.X),
                      reads=[rPS[sb_]], writes=[rSM[a]])
                sc.op(DVE, lambda: nc.vector.tensor_tensor(sm[:, 0:2], sm[:, 0:2], SINK8[:, 2 * c:2 * c + 2], ALU.max),
                      reads=[rSM[a], rSINK], writes=[rSM[a]])
                sc.op(DVE, lambda: nc.vector.tensor_scalar(sm[:, 2:4], sm[:, 0:2], -0.125, None, ALU.mult),
                      reads=[rSM[a]], writes=[rSM[a]])
                sc.op(DVE, lambda: nc.vector.memset(sm[:, 4:6], 0.0), writes=[rSM[a]])
                for hh in range(2):
                    sc.op(ACT, lambda hh=hh: nc.scalar.activation(PF[a][:, hh * 256:(hh + 1) * 256], PS[sb_][:, hh * 256:(hh + 1) * 256],
                                                                  AF.Exp, bias=sm[:, 2 + hh:3 + hh], scale=0.125,
                                                                  accum_out=sm[:, 4 + hh:5 + hh]),
                          reads=[rPS[sb_], rSM[a]], writes=[rPF[a], rSM[a]])
                    sc.op(ACT, lambda hh=hh: nc.scalar.activation(sm[:, 6 + hh:7 + hh], vcol("sinks", 2 * c + hh, 1), AF.Exp,
                                                                  bias=sm[:, 2 + hh:3 + hh], scale=1.0),
                          reads=[rSM[a], rVEC], writes=[rSM[a]])

            def emit_S2(c, qb, a):
                sm = SM[a]
                sc.op(DVE, lambda: nc.vector.tensor_tensor(sm[:, 8:10], sm[:, 4:6], sm[:, 6:8], ALU.add),
                      reads=[rSM[a]], writes=[rSM[a]])
                sc.op(DVE, lambda: nc.vector.reciprocal(sm[:, 8:10], sm[:, 8:10]), reads=[rSM[a]], writes=[rSM[a]])
                for hh in range(2):
                    sc.op(DVE, lambda hh=hh: nc.vector.tensor_scalar(PN[a][:, hh * 256:(hh + 1) * 256], PF[a][:, hh * 256:(hh + 1) * 256],
                                                                     sm[:, 8 + hh:9 + hh], None, ALU.mult),
                          reads=[rPF[a], rSM[a]], writes=[rPN[a]])

            def emit_T(c, qb, a):
                fns = [lambda k=k: nc.tensor.matmul(PS[4 + a][:, k * 128:(k + 1) * 128], PN[a][:, k * 128:(k + 1) * 128], IDN[:, :],
                                                    start=True, stop=True)
                       for k in range(4)]
                sc.group(PE, fns, reads=[rPN[a], rCST], writes=[rPS[4 + a]])
                sc.op(ACT, lambda: nc.scalar.copy(PTS[a][:, :], PS[4 + a][:, 0:512]), reads=[rPS[4 + a]], writes=[rPTS[a]])

            def emit_PV(c, qb, a):
                g = c // 4
                q0 = qb * 128
                ob = (2, 3) if c % 2 == 0 else (6, 7)
                fns = []
                for hh in range(2):
                    for kb in range(2):
                        vs = qb + kb
                        fns.append(lambda hh=hh, kb=kb, vs=vs:
                                   nc.tensor.matmul(PS[ob[hh]][0:64, q0:q0 + 128], V[:, vs, g * 64:(g + 1) * 64],
                                                    PTS[a][:, (hh * 2 + kb) * 128:(hh * 2 + kb + 1) * 128],
                                                    start=(kb == 0), stop=(kb == 1)))
                sc.group(PE, fns, reads=[rPTS[a], rV[qb], rV[qb + 1]], writes=[rPS[ob[0]], rPS[ob[1]]])
                if qb == 3:
                    sc.op(ACT, lambda: nc.scalar.copy(big_bf(16 + c, T)[0:64, :], PS[ob[0]][0:64, 0:T]), reads=[rPS[ob[0]]], writes=[rBIG[16 + c]])
                    sc.op(ACT, lambda: nc.scalar.copy(big_bf(16 + c, T)[64:128, :], PS[ob[1]][0:64, 0:T]), reads=[rPS[ob[1]]], writes=[rBIG[16 + c]])

            pairs = [(c, qb, (c * 4 + qb) % 2) for c in range(KC) for qb in range(4)]
            NP_ = len(pairs)
            for pi in range(NP_ + 3):
                if pi < NP_:
                    emit_S1(*pairs[pi])
                if 1 <= pi <= NP_:
                    emit_S2(*pairs[pi - 1])
                if 2 <= pi <= NP_ + 1:
                    emit_T(*pairs[pi - 2])
                if pi >= 3:
                    emit_PV(*pairs[pi - 3])

            if stop == "att":
                sc.dma(POOL, outT[:, 0:T].rearrange("(c p) t -> p c t", p=128),
                       X[:, 0:KC * T].rearrange("p (c t) -> p c t", t=T), osem, ocnt, reads=rX + [rBIG[g_] for g_ in range(32)])
                nc.gpsimd.wait_ge(osem, ocnt[0])
                return nc
            sc.op(DVE, lambda: nc.vector.tensor_copy(KT[:, :, :, 0:128], KT[:, :, :, 512:640]), reads=[rKT], writes=[rKT])
            sc.op(DVE, lambda: nc.vector.tensor_copy(V[:, 0, :], V[:, 4, :]), reads=[rV[4]], writes=[rV[0]])

            proj_fm("wo", 2048, D, [big_bf(16 + c, T) for c in range(KC)], [rBIG[16 + c] for c in range(KC)], T,
                    post_epi_factory(lambda oc: big_f32(oc, T, base_g=32), lambda oc: rbig_f32(oc, 32), T,
                                     biascol=lambda oc: vcol("bo", oc, 1)))
            postnorm_apply(lambda c: big_f32(c, T, base_g=32), lambda c: rbig_f32(c, 32), T, Gcol(1, 0))

            mlp(1, T, final=True)

            j = tix - 1
            obig = BIG[:, :].bitcast(F32)[:, 0:KC * T].rearrange("p (c t) -> p c t", t=T)
            sc.dma(POOL, outT[:, j * TM:(j + 1) * TM].rearrange("(c p) t -> p c t", p=128),
                   obig, osem, ocnt, reads=[rBIG[g_] for g_ in range(32)])

        nc.gpsimd.wait_ge(osem, ocnt[0])
    return nc


def _rope_tables(p0):
    inv = (np.float32(500000.0) ** (-np.arange(0, 16, 2, dtype=np.float32) / np.float32(16))).astype(np.float32)
    pos = (np.arange(LTOK, dtype=np.float32) + np.float32(p0 - HALO)).astype(np.float32)
    ang = (pos[:, None] * inv[None, :]).astype(np.float32)
    cos = np.cos(ang).astype(np.float32)
    sin = np.sin(ang).astype(np.float32)
    C = np.ones((128, LTOK), np.float32)
    Sg = np.zeros((128, LTOK), np.float32)
    for r in range(128):
        j = r % 64
        q = j // 32
        jj = j % 32
        if jj < 8:
            C[r] = cos[:, jj]
            Sg[r] = -sin[:, jj] if q == 0 else sin[:, jj]
    Ssw = np.ascontiguousarray(Sg.reshape(2, 2, 32, LTOK)[:, ::-1].reshape(128, LTOK))
    return C, Ssw


def _masks(first_core_half):
    i = np.arange(128)[:, None]
    j = np.arange(128)[None, :]
    mp = np.where(j > i, 0.0, NEG).astype(np.float32)
    mc = np.where(j <= i, 0.0, NEG).astype(np.float32)
    m2 = np.concatenate([mp, mc, mp, mc], axis=1)
    if first_core_half:
        mp0 = np.full((128, 128), NEG, np.float32)
    else:
        mp0 = mp
    m0 = np.concatenate([mp0, mc, mp0, mc], axis=1)
    return m2, m0


def _prep(inputs, n_cores=8):
    f = lambda a: np.ascontiguousarray(np.asarray(a, dtype=np.float32))
    x = f(inputs["x"])
    c = f(inputs["c"])
    ada_w = f(inputs["ada_w"])
    ada_b = f(inputs["ada_b"])
    norm_pre = f(inputs["norm_pre"])
    norm_post = f(inputs["norm_post"])
    conv_w = f(inputs["conv_w"])[0]
    w_kv = f(inputs["w_kv"])
    b_kv = f(inputs["b_kv"])
    w_q = f(inputs["w_q"])[0]
    b_q = f(inputs["b_q"])[0]
    sinks = f(inputs["sinks"])[0]
    b_o = f(inputs["b_o"])[0]

    qcols = np.concatenate([h * 64 + PERM for h in range(NH)])
    w_q_p = np.ascontiguousarray(w_q[:, qcols])
    b_q_p = b_q[qcols]
    kcols = np.concatenate([np.concatenate([g * 64 + PERM, g * 64 + PERM]) for g in range(4)])
    w_kdup = np.ascontiguousarray(w_kv[:, kcols])
    b_kdup = b_kv[kcols]
    w_v = np.ascontiguousarray(w_kv[:, 256:512])
    b_v = b_kv[256:512]

    shared = {
        "ada_w": ada_w, "kv_ada_w": f(inputs["kv_ada_w"]), "w_in": f(inputs["conv_w_in"])[0],
        "w_out": f(inputs["conv_w_out"])[0], "w_up": f(inputs["mlp_up"]), "w_down": f(inputs["mlp_down"]),
        "w_kdup": w_kdup, "w_v": w_v, "w_q": w_q_p, "w_o": f(inputs["w_o"])[0],
    }
    ident = np.eye(128, dtype=np.float32)
    in_maps = []
    for r in range(n_cores):
        b, hf = r // 2, r % 2
        p0 = hf * TOK
        xt = np.zeros((D, LTOK), np.float32)
        lo = p0 - HALO
        if lo >= 0:
            xt[:, :] = x[b, lo:p0 + TOK, :].T
        else:
            xt[:, HALO:] = x[b, 0:TOK, :].T
        vec = np.zeros((128, NV), np.float32)

        def put(name, arr):
            o, w = VOFF[name]
            assert arr.shape == (128, w), (name, arr.shape, w)
            vec[:, o:o + w] = arr
        put("c", _col(c[b]))
        for l in range(2):
            for i in range(2):
                put(f"ada_b{l}{i}", _col(ada_b[l, i]))
                put(f"gpre{l}{i}", _col(norm_pre[l, i]))
                put(f"gpost{l}{i}", _col(norm_post[l, i]))
        put("kv_ada_b", _col(inputs["kv_ada_b"]))
        put("kv_norm", _col(inputs["kv_norm"]))
        put("cw0", _col(conv_w[0]))
        put("cw1", _col(conv_w[1]))
        put("cw2", _col(conv_w[2]))
        put("bk", _col(b_kdup))
        put("bq", _col(b_q_p))
        put("bo", _col(b_o))
        put("sinks", np.broadcast_to(sinks[None, :], (128, 32)))
        put("flag", np.full((128, 1), float(hf), np.float32))
        put("bv", np.broadcast_to(b_v[None, :], (128, 256)))
        C, Sg = _rope_tables(p0)
        m2, m0 = _masks(hf == 0)
        cst = np.concatenate([ident, m2, m0], axis=1).astype(np.float32)
        m = {"xT": xt, "vecs": vec, "ropeC": C, "ropeS": Sg, "cst": cst}
        m.update(shared)
        in_maps.append(m)
    return in_maps


def kernel(**inputs):
    n = 8
    in_maps = _prep(inputs, n)
    nc = build_nc()
    res = run_bass_kernel_spmd(nc, in_maps, core_ids=list(range(n)))
    out = np.empty((NB, SEQ, D), np.float32)
    for r in range(n):
        b, hf = r // 2, r % 2
        out[b, hf * TOK:(hf + 1) * TOK, :] = res.results[r]["outT"].T
    return out
```
